# Optimizing a Trainium2 kernel written in Bass

```python
import math
import jax, jax.numpy as jnp
from jax import lax
import numpy as np

D_MODEL = 1024
BATCH = 4
SEQ = 4096
DEPTH = 2

GRID_W = 64
CTX_LEN = 256
HEAD_DIM = 64
ROPE_FREQS = HEAD_DIM // 4
ROPE_THETA = 10000.0
Q_BLOCK = 128
EPS = 1e-6
ATTN_SCALE = HEAD_DIM ** -0.5

FOURIER_GROUP_DIM = 64
FOURIER_GROUPS = (3 * D_MODEL // 8) // FOURIER_GROUP_DIM
FOURIER_WIDTH = FOURIER_GROUPS * FOURIER_GROUP_DIM
DIFF_HEADS = D_MODEL // 256
DIFF_QK_WIDTH = DIFF_HEADS * 2 * HEAD_DIM
DIFF_V_DIM = 2 * HEAD_DIM
DIFF_V_WIDTH = DIFF_HEADS * DIFF_V_DIM
GQA_Q_HEADS = D_MODEL // 128
GQA_GROUP = 4
GQA_KV_HEADS = GQA_Q_HEADS // GQA_GROUP
GQA_Q_WIDTH = GQA_Q_HEADS * HEAD_DIM
GQA_KV_WIDTH = GQA_KV_HEADS * HEAD_DIM
CONV_WIDTH = 3 * D_MODEL // 8
CONV_KERNEL = 31
N_BRANCHES = 4
D_FF = 4 * D_MODEL
N_MOD = 6

KV_SIZES = (DIFF_QK_WIDTH, DIFF_V_WIDTH, GQA_KV_WIDTH, GQA_KV_WIDTH)
REST_SIZES = (FOURIER_WIDTH, DIFF_QK_WIDTH, GQA_Q_WIDTH, 2 * CONV_WIDTH, N_BRANCHES * D_MODEL)
KV_COLS = sum(KV_SIZES)
IN_COLS = KV_COLS + sum(REST_SIZES)

kernel_name = "hybrid_parallel_dit_block"


def _split(t, sizes):
    idx = np.cumsum(sizes)[:-1].tolist()
    return jnp.split(t, idx, axis=-1)


def _rms_norm(x, g):
    xf = x.astype(jnp.float32)
    y = xf * lax.rsqrt(jnp.mean(xf * xf, axis=-1, keepdims=True) + EPS)
    return (y * g.astype(jnp.float32)).astype(x.dtype)


def _layer_norm(x, g, b):
    xf = x.astype(jnp.float32)
    mu = jnp.mean(xf, axis=-1, keepdims=True)
    var = jnp.mean(jnp.square(xf - mu), axis=-1, keepdims=True)
    y = (xf - mu) * lax.rsqrt(var + EPS) * g.astype(jnp.float32) + b.astype(jnp.float32)
    return y.astype(x.dtype)


def _modulate(x, g, shift, scale):
    return _rms_norm(x, g) * (1 + scale) + shift


def _axial_rope_tables(seq_len):
    rows = seq_len // GRID_W
    row = jnp.repeat(jnp.arange(rows, dtype=jnp.float32), GRID_W)
    col = jnp.tile(jnp.arange(GRID_W, dtype=jnp.float32), rows)
    inv_freq = ROPE_THETA ** (-jnp.arange(ROPE_FREQS, dtype=jnp.float32) / ROPE_FREQS)
    ang = jnp.stack([row[:, None] * inv_freq, col[:, None] * inv_freq], axis=1)
    return jnp.cos(ang), jnp.sin(ang)


def _apply_rope(x, cos, sin):
    xf = x.astype(jnp.float32).reshape(x.shape[:-1] + (2, 2, ROPE_FREQS))
    bshape = (cos.shape[0],) + (1,) * (x.ndim - 3) + cos.shape[1:]
    cs, sn = cos.reshape(bshape), sin.reshape(bshape)
    re, im = xf[..., 0, :], xf[..., 1, :]
    out = jnp.stack([re * cs - im * sn, im * cs + re * sn], axis=-2)
    return out.reshape(x.shape).astype(x.dtype)


def _kv_heads(kv, k_norm):
    b, l = kv.shape[:2]
    dk, dv, gk, gv = _split(kv, KV_SIZES)
    dk = dk.reshape(b, l, DIFF_HEADS, 2, HEAD_DIM)
    dv = dv.reshape(b, l, DIFF_HEADS, DIFF_V_DIM)
    gk = _rms_norm(gk.reshape(b, l, GQA_KV_HEADS, HEAD_DIM), k_norm)
    gv = gv.reshape(b, l, GQA_KV_HEADS, HEAD_DIM)
    return dk, dv, gk, gv


def _diff_attention(dq, dk, dv, lam):
    s = jnp.einsum('bqhjd,bkhjd->bhjqk', dq, dk).astype(jnp.float32) * ATTN_SCALE
    p = jax.nn.softmax(s, axis=-1)
    a = (p[:, :, 0] - lam * p[:, :, 1]).astype(dv.dtype)
    return jnp.einsum('bhqk,bkhe->bqhe', a, dv)


def _gqa_attention(gq, gk, gv):
    s = jnp.einsum('bqhgd,bkhd->bhgqk', gq, gk).astype(jnp.float32) * ATTN_SCALE
    p = jax.nn.softmax(s, axis=-1).astype(gv.dtype)
    return jnp.einsum('bhgqk,bkhd->bqhgd', p, gv)


def _sweep_query_blocks(fn, qs):
    b, l = qs[0].shape[:2]
    n = l // Q_BLOCK
    def to_blocks(q):
        return jnp.moveaxis(q.reshape((b, n, Q_BLOCK) + q.shape[2:]), 1, 0)
    out = lax.map(lambda qb: fn(*qb), tuple(to_blocks(q) for q in qs))
    def from_blocks(o):
        return jnp.moveaxis(o, 0, 1).reshape((b, l) + o.shape[3:])
    return jax.tree_util.tree_map(from_blocks, out)


def _fourier_mix(u):
    b, l = u.shape[:2]
    ug = u.astype(jnp.float32).reshape(b, l, FOURIER_GROUPS, FOURIER_GROUP_DIM)
    y = jnp.fft.fft2(ug, axes=(1, 3), norm='ortho').real
    return y.reshape(b, l, FOURIER_WIDTH).astype(u.dtype)


def _conformer_conv(u, dw, dw_bias, ln_g, ln_b):
    a, g = jnp.split(u, 2, axis=-1)
    v = a * jax.nn.sigmoid(g)
    y = lax.conv_general_dilated(
        v, dw[:, None, :], window_strides=(1,),
        padding=[(CONV_KERNEL // 2, CONV_KERNEL // 2)],
        dimension_numbers=('NWC', 'WIO', 'NWC'),
        feature_group_count=CONV_WIDTH) + dw_bias
    return jax.nn.silu(_layer_norm(y, ln_g, ln_b))


def _token_mixer(h, kv_ctx, rope, lp, lam, lam_init):
    b, l = h.shape[:2]
    w_in = lp['w_in']
    if rope is None:
        rest = h @ w_in[:, KV_COLS:]
        dk, dv, gk, gv = kv_ctx
    else:
        proj = h @ w_in
        dk, dv, gk, gv = _kv_heads(proj[..., :KV_COLS], lp['k_norm'])
        rest = proj[..., KV_COLS:]
    u_f, dq, gq, u_c, gate_logits = _split(rest, REST_SIZES)
    dq = dq.reshape(b, l, DIFF_HEADS, 2, HEAD_DIM)
    gq = _rms_norm(gq.reshape(b, l, GQA_KV_HEADS, GQA_GROUP, HEAD_DIM), lp['q_norm'])
    if rope is None:
        o_d = _diff_attention(dq, dk, dv, lam)
        o_g = _gqa_attention(gq, gk, gv)
    else:
        cos, sin = rope
        dq, dk = _apply_rope(dq, cos, sin), _apply_rope(dk, cos, sin)
        gq, gk = _apply_rope(gq, cos, sin), _apply_rope(gk, cos, sin)
        cdk, cdv, cgk, cgv = kv_ctx
        dk_all = jnp.concatenate([dk, cdk], axis=1)
        dv_all = jnp.concatenate([dv, cdv], axis=1)
        gk_all = jnp.concatenate([gk, cgk], axis=1)
        gv_all = jnp.concatenate([gv, cgv], axis=1)
        o_d, o_g = _sweep_query_blocks(
            lambda qd, qg: (_diff_attention(qd, dk_all, dv_all, lam), _gqa_attention(qg, gk_all, gv_all)),
            (dq, gq))
    o_d = (_rms_norm(o_d, lp['subln']) * (1.0 - lam_init)).reshape(b, l, DIFF_V_WIDTH)
    o_g = o_g.reshape(b, l, GQA_Q_WIDTH)
    y_f = _fourier_mix(u_f)
    y_c = _conformer_conv(u_c, lp['conv_dw'], lp['conv_dw_bias'], lp['conv_ln_g'], lp['conv_ln_b'])
    g_f, g_d, g_g, g_c = jnp.split(jax.nn.sigmoid(gate_logits), N_BRANCHES, axis=-1)
    merged = (g_f * (y_f @ lp['w_br_fourier']) + g_d * (o_d @ lp['w_br_diff'])
              + g_g * (o_g @ lp['w_br_gqa']) + g_c * (y_c @ lp['w_br_conv']))
    return merged @ lp['w_out']


def _sq_relu_mlp(h, w1, w2):
    return jnp.square(jax.nn.relu(h @ w1)) @ w2


def setup_inputs(seed: int = 0) -> dict:
    key = jax.random.key(seed)
    ks = jax.random.split(key, 32)
    f32 = jnp.float32
    def nrm(k, shape, scale):
        return jax.random.normal(k, shape, f32) * scale
    def gain(k, shape):
        return 1.0 + 0.02 * jax.random.normal(k, shape, f32)
    D = D_MODEL
    return {
        "x": nrm(ks[0], (BATCH, SEQ, D), 1.0),
        "c": nrm(ks[1], (BATCH, D), 1.0),
        "ctx": nrm(ks[2], (BATCH, CTX_LEN, D), 1.0),
        "c_ctx": nrm(ks[3], (D,), 1.0),
        "w_mod": nrm(ks[4], (DEPTH, D, N_MOD * D), 0.5 * D ** -0.5),
        "b_mod": nrm(ks[5], (DEPTH, N_MOD * D), 0.01),
        "g_pre_mix": gain(ks[6], (DEPTH, D)),
        "g_post_mix": gain(ks[7], (DEPTH, D)),
        "g_pre_mlp": gain(ks[8], (DEPTH, D)),
        "g_post_mlp": gain(ks[9], (DEPTH, D)),
        "w_in": nrm(ks[10], (DEPTH, D, IN_COLS), D ** -0.5),
        "q_norm": gain(ks[11], (DEPTH, HEAD_DIM)),
        "k_norm": gain(ks[12], (DEPTH, HEAD_DIM)),
        "diff_lambda": nrm(ks[13], (DEPTH, 4, HEAD_DIM), 0.1),
        "diff_subln": gain(ks[14], (DEPTH, DIFF_V_DIM)),
        "conv_dw": nrm(ks[15], (DEPTH, CONV_KERNEL, CONV_WIDTH), CONV_KERNEL ** -0.5),
        "conv_dw_bias": nrm(ks[16], (DEPTH, CONV_WIDTH), 0.01),
        "conv_ln_g": gain(ks[17], (DEPTH, CONV_WIDTH)),
        "conv_ln_b": nrm(ks[18], (DEPTH, CONV_WIDTH), 0.01),
        "w_br_fourier": nrm(ks[19], (DEPTH, FOURIER_WIDTH, D), FOURIER_WIDTH ** -0.5),
        "w_br_diff": nrm(ks[20], (DEPTH, DIFF_V_WIDTH, D), DIFF_V_WIDTH ** -0.5),
        "w_br_gqa": nrm(ks[21], (DEPTH, GQA_Q_WIDTH, D), GQA_Q_WIDTH ** -0.5),
        "w_br_conv": nrm(ks[22], (DEPTH, CONV_WIDTH, D), CONV_WIDTH ** -0.5),
        "w_out": nrm(ks[23], (DEPTH, D, D), D ** -0.5),
        "w_ff1": nrm(ks[24], (DEPTH, D, D_FF), D ** -0.5),
        "w_ff2": nrm(ks[25], (DEPTH, D_FF, D), D_FF ** -0.5),
    }


def reference(x, c, ctx, c_ctx, w_mod, b_mod, g_pre_mix, g_post_mix, g_pre_mlp, g_post_mlp,
              w_in, q_norm, k_norm, diff_lambda, diff_subln, conv_dw, conv_dw_bias, conv_ln_g, conv_ln_b,
              w_br_fourier, w_br_diff, w_br_gqa, w_br_conv, w_out, w_ff1, w_ff2):
    rope = _axial_rope_tables(x.shape[1])
    xc = ctx
    for l in range(DEPTH):
        lp = {
            'w_in': w_in[l], 'q_norm': q_norm[l], 'k_norm': k_norm[l], 'subln': diff_subln[l],
            'conv_dw': conv_dw[l], 'conv_dw_bias': conv_dw_bias[l],
            'conv_ln_g': conv_ln_g[l], 'conv_ln_b': conv_ln_b[l],
            'w_br_fourier': w_br_fourier[l], 'w_br_diff': w_br_diff[l],
            'w_br_gqa': w_br_gqa[l], 'w_br_conv': w_br_conv[l], 'w_out': w_out[l],
        }
        lam_init = 0.8 - 0.6 * math.exp(-0.3 * l)
        lv = diff_lambda[l].astype(jnp.float32)
        lam = jnp.exp(jnp.sum(lv[0] * lv[1])) - jnp.exp(jnp.sum(lv[2] * lv[3])) + lam_init

        mod_x = (jax.nn.silu(c) @ w_mod[l] + b_mod[l])[:, None, :]
        mod_c = (jax.nn.silu(c_ctx) @ w_mod[l] + b_mod[l])[None, None, :]
        sx1, cx1, gx1, sx2, cx2, gx2 = jnp.split(mod_x, N_MOD, axis=-1)
        sc1, cc1, gc1, sc2, cc2, gc2 = jnp.split(mod_c, N_MOD, axis=-1)

        hc = _modulate(xc, g_pre_mix[l], sc1, cc1)
        kv_ctx = _kv_heads(hc @ w_in[l][:, :KV_COLS], k_norm[l])

        h = _modulate(x, g_pre_mix[l], sx1, cx1)
        x = x + gx1 * _rms_norm(_token_mixer(h, kv_ctx, rope, lp, lam, lam_init), g_post_mix[l])
        hm = _modulate(x, g_pre_mlp[l], sx2, cx2)
        x = x + gx2 * _rms_norm(_sq_relu_mlp(hm, w_ff1[l], w_ff2[l]), g_post_mlp[l])

        if l < DEPTH - 1:
            xc = xc + gc1 * _rms_norm(_token_mixer(hc, kv_ctx, None, lp, lam, lam_init), g_post_mix[l])
            hcm = _modulate(xc, g_pre_mlp[l], sc2, cc2)
            xc = xc + gc2 * _rms_norm(_sq_relu_mlp(hcm, w_ff1[l], w_ff2[l]), g_post_mlp[l])
    return x
```

```python
import math
import contextlib
import numpy as np
import ml_dtypes
import concourse.bass as bass
import concourse.mybir as mybir
from concourse.bass_utils import run_bass_kernel_spmd

F32 = mybir.dt.float32
BF16 = mybir.dt.bfloat16
U8 = mybir.dt.uint8
AF = mybir.ActivationFunctionType
ALU = mybir.AluOpType
AX = mybir.AxisListType

D = 1024
DEPTH = 2
T_OWN = 2048
T_LAT = 4096
T_CTX = 256
NT_ALL = 34
NKEY = 4352
EPS = 1e-6
IN_COLS = 7552
GATE_COL0 = 3456
ARENA = 206 * 1024

ENGS = ("pe", "act", "dve", "pool", "sp")
NSEM_ENG = 4
NSEM_DMA = 12


class Op:
    __slots__ = ("eng", "fn", "dma", "deps", "sig", "sig_idx", "dma_idx")

    def __init__(self, eng, fn, dma):
        self.eng = eng
        self.fn = fn
        self.dma = dma
        self.deps = []
        self.sig = False
        self.sig_idx = -1
        self.dma_idx = -1


class Sched:
    def __init__(self):
        self.ops = {e: [] for e in ENGS}
        self.state = {}
        self.fence = None
        self.dmas_since = []
        self.last = {}

    def _st(self, k):
        s = self.state.get(k)
        if s is None:
            s = [[], []]
            self.state[k] = s
        return s

    def op(self, eng, fn, reads=(), writes=(), dma=False, extra=()):
        o = Op(eng, fn, dma)
        deps = {}
        for k in reads:
            s = self.state.get(k)
            if s:
                for w in s[0]:
                    deps[id(w)] = w
        accum = set()
        for k in writes:
            s = self.state.get(k)
            if s:
                if dma and not s[1] and s[0] and all(w.dma for w in s[0]):
                    accum.add(k)
                    continue
                for w in s[0]:
                    deps[id(w)] = w
                for r in s[1]:
                    deps[id(r)] = r
        for d in extra:
            deps[id(d)] = d
        if self.fence is not None:
            deps[id(self.fence)] = self.fence
        for d in deps.values():
            if (not d.dma) and (not dma) and d.eng == eng and eng == "pe":
                continue
            o.deps.append(d)
            d.sig = True
        for k in reads:
            rl = self._st(k)[1]
            if not dma:
                rl[:] = [x for x in rl if x.dma or x.eng != eng]
            rl.append(o)
        for k in writes:
            s = self._st(k)
            if k in accum:
                s[0] = s[0] + [o]
            else:
                s[0] = [o]
            s[1] = []
        self.ops[eng].append(o)
        if dma:
            self.dmas_since.append(o)
        else:
            self.last[eng] = o
        return o

    def pe(self, fn, r=(), w=()):
        return self.op("pe", fn, r, w)

    def act(self, fn, r=(), w=()):
        return self.op("act", fn, r, w)

    def dve(self, fn, r=(), w=()):
        return self.op("dve", fn, r, w)

    def pool(self, fn, r=(), w=()):
        return self.op("pool", fn, r, w)

    def dma(self, out, in_, r=(), w=(), q="sp"):
        return self.op(q, lambda e: e.dma_start(out=out, in_=in_), r, w, dma=True)

    def barrier(self, scratch):
        import os
        v = os.environ.get("BARV", "")
        extra = list(self.last.values()) + list(self.dmas_since)
        if v == "nodma":
            extra = list(self.last.values())
        if v == "onlydma":
            extra = list(self.dmas_since)
        self.fence = None
        o = self.op("dve", lambda e: e.memset(scratch, 0.0), (), (), extra=extra)
        o.sig = True
        self.fence = o
        self.dmas_since = []
        self.state = {}

    def emit(self, nc):
        for e in ENGS:
            si = 0
            di = 0
            for o in self.ops[e]:
                if o.dma:
                    o.dma_idx = di
                    di += 1
                elif o.sig:
                    o.sig_idx = si
                    si += 1
        with contextlib.ExitStack() as es:
            esem = {e: [es.enter_context(nc.semaphore(f"s_{e}_{i}")) for i in range(NSEM_ENG)]
                    for e in ("pe", "act", "dve", "pool")}
            dsem = {e: [es.enter_context(nc.semaphore(f"d_{e}_{i}")) for i in range(NSEM_DMA)]
                    for e in ("sp", "pool", "act")}
            block = es.enter_context(nc.Block())

            def run(e, eng):
                known = {x: -1 for x in ("pe", "act", "dve", "pool")}
                dknown = {}
                for o in self.ops[e]:
                    for d in o.deps:
                        if d.dma:
                            slot = d.dma_idx % NSEM_DMA
                            val = 16 * (d.dma_idx // NSEM_DMA + 1)
                            if dknown.get((d.eng, slot), 0) >= val:
                                continue
                            eng.wait_ge(dsem[d.eng][slot], val)
                            dknown[(d.eng, slot)] = val
                        else:
                            if known[d.eng] >= d.sig_idx:
                                continue
                            eng.wait_ge(esem[d.eng][d.sig_idx % NSEM_ENG], d.sig_idx // NSEM_ENG + 1)
                            known[d.eng] = d.sig_idx
                    if o.dma:
                        slot = o.dma_idx % NSEM_DMA
                        prev = 16 * (o.dma_idx // NSEM_DMA)
                        if prev > 0 and dknown.get((e, slot), 0) < prev:
                            eng.wait_ge(dsem[e][slot], prev)
                            dknown[(e, slot)] = prev
                        ins = o.fn(eng)
                        ins.then_inc(dsem[e][slot], 16)
                    else:
                        ins = o.fn(eng)
                        if o.sig:
                            ins.then_inc(esem[e][o.sig_idx % NSEM_ENG], 1)
                nd = sum(1 for o in self.ops[e] if o.dma)
                for slot in range(min(nd, NSEM_DMA)):
                    cnt = (nd - slot + NSEM_DMA - 1) // NSEM_DMA
                    if dknown.get((e, slot), 0) < 16 * cnt:
                        eng.wait_ge(dsem[e][slot], 16 * cnt)

            @block.tensor
            def _(eng):
                run("pe", eng)

            @block.scalar
            def _(eng):
                run("act", eng)

            @block.vector
            def _(eng):
                run("dve", eng)

            @block.gpsimd
            def _(eng):
                run("pool", eng)

            @block.sync
            def _(eng):
                run("sp", eng)


I32 = mybir.dt.int32


def _dsize(dt):
    return 4 if dt in (F32, I32) else 2


class Mem:
    def __init__(self, arena, size):
        self.a = arena
        self.off = 0
        self.size = size

    def alloc(self, shape, dt):
        n = int(np.prod(shape))
        nb = n * _dsize(dt)
        start = self.off
        self.off += (nb + 63) // 64 * 64
        assert self.off <= self.size, f"SBUF arena overflow {self.off} > {self.size}"
        v = self.a[:, start:start + nb].bitcast(dt)
        return view(v, shape)

    def mark(self):
        return self.off

    def release(self, m):
        self.off = m


def view(v, shape):
    if len(shape) == 1:
        return v
    names = "abcde"[:len(shape)]
    kw = {names[i]: int(shape[i]) for i in range(len(shape) - 1)}
    return v.rearrange(f"p ({' '.join(names)}) -> p {' '.join(names)}", **kw)


class Ring:
    def __init__(self, mem, n, shape, dt, name):
        self.bufs = [mem.alloc(shape, dt) for _ in range(n)]
        self.n = n
        self.i = 0
        self.name = name

    def next(self):
        j = self.i % self.n
        self.i += 1
        return self.bufs[j], f"{self.name}{j}"


class PRing:
    def __init__(self, banks, idxs):
        self.banks = banks
        self.idxs = idxs
        self.i = 0

    def next(self):
        j = self.idxs[self.i % len(self.idxs)]
        self.i += 1
        return self.banks[j], f"pb{j}"


class Builder:
    def __init__(self, nc):
        self.nc = nc
        self.S = Sched()
        S = self.S
        self.arena = nc.alloc_sbuf_tensor("arena", [128, ARENA], U8)
        self.mem = Mem(self.arena, ARENA)
        self.pp = [nc.alloc_psum_tensor(f"pp{i}", [128, 1024], F32).ap() for i in range(4)]
        self.pb = [self.pp[i // 2][:, (i % 2) * 512:(i % 2 + 1) * 512] for i in range(8)]
        m = self.mem
        self.ident = m.alloc([128], BF16)
        self.cbd = m.alloc([128], BF16)
        self.nsbd = m.alloc([128], BF16)
        self.masks = m.alloc([4], F32)
        self.rowsB = m.alloc([1664], F32)
        self.fmv = m.alloc([109], F32)
        self.AB = m.alloc([2, 4, 8], F32)
        self.lamt = m.alloc([8], F32)
        self.sublnS = m.alloc([128], F32)
        self.bar = m.alloc([16], F32)
        self.bar2 = m.alloc([16], F32)
        self.idx_s = m.alloc([33], I32)
        self.tok_s = m.alloc([64], I32)
        self.hT = m.alloc([8, 2304], BF16)
        self.base = m.mark()

    def MM(self, out, lhsT, rhs, start, stop, r, w):
        self.S.pe(lambda e: e.matmul(out, lhsT=lhsT, rhs=rhs, start=start, stop=stop), r, w)

    def TR(self, out, in_, r, w):
        ident = self.ident
        self.S.pe(lambda e: e.transpose(out, in_, ident), list(r) + ["ident"], w)

    def ACT(self, out, in_, func, r, w, scale=None, bias=None, accum=None):
        kw = {}
        if scale is not None:
            kw["scale"] = scale
        if bias is not None:
            kw["bias"] = bias
        if accum is not None:
            kw["accum_out"] = accum
        self.S.act(lambda e: e.activation(out=out, in_=in_, func=func, **kw), r, w)

    def TT(self, eng, out, in0, in1, op, r, w):
        self.S.op(eng, lambda e: e.tensor_tensor(out=out, in0=in0, in1=in1, op=op), r, w)

    def TS(self, eng, out, in0, s1, s2, op0, op1, r, w):
        if s2 is None:
            self.S.op(eng, lambda e: e.tensor_scalar(out=out, in0=in0, scalar1=s1, scalar2=None, op0=op0), r, w)
        else:
            self.S.op(eng, lambda e: e.tensor_scalar(out=out, in0=in0, scalar1=s1, scalar2=s2, op0=op0, op1=op1), r, w)

    def STT(self, eng, out, in0, scalar, in1, op0, op1, r, w):
        self.S.op(eng, lambda e: e.scalar_tensor_tensor(out=out, in0=in0, scalar=scalar, in1=in1, op0=op0, op1=op1), r, w)

    def CP(self, eng, out, in_, r, w):
        if eng == "act":
            self.S.act(lambda e: e.activation(out=out, in_=in_, func=AF.Copy), r, w)
        else:
            self.S.op(eng, lambda e: e.tensor_copy(out=out, in_=in_), r, w)

    def RECIP(self, out, in_, r, w):
        self.S.dve(lambda e: e.reciprocal(out=out, in_=in_), r, w)

    def rstd(self, ss, n, out, tmp, r, w):
        self.TS("dve", tmp, ss, 1.0 / n, EPS, ALU.mult, ALU.add, r, w)
        self.ACT(tmp, tmp, AF.Sqrt, w, w)
        self.RECIP(out, tmp, w, w)

    def barrier(self):
        self.S.barrier(self.bar)

    def dump(self, name, ap):
        import os
        if os.environ.get("DBG", "") == "":
            return
        t = self.nc.dram_tensor("dbg_" + name, [128] + [int(x) for x in ap.shape[1:]], ap.dtype, kind="ExternalOutput").ap()
        self.S.dma(t, ap)
        self.barrier()

    def load_consts(self, io):
        S = self.S
        S.dma(self.ident, io["ident"], w=["ident"])
        S.dma(self.cbd, io["cbd"], w=["cbd"])
        S.dma(self.nsbd, io["nsbd"], w=["nsbd"])
        if "idxs" in io:
            S.dma(self.idx_s, io["idxs"], w=["idx_s"])
            S.dma(self.tok_s, io["tok"], w=["tok_s"])

    def layer(self, l, io, x_all, x_out, xc_out, scr, ctx_full, stop_after=None, tabs=None, reuse_kv=False, skip_setup=False, scatter_to=None, gather_from=None):
        S = self.S
        m = self.mem
        pb = self.pb
        lam_init = 0.8 - 0.6 * math.exp(-0.3 * l)
        n_slots = 18 if ctx_full else 16
        hT = self.hT
        if tabs is None:
            tabs = io

        def pbf(i):
            return pb[i].bitcast(BF16)

        self.barrier()
        mk_layer = m.mark()
        hTh = m.alloc([8, 256], BF16)
        vT = m.alloc([3, 2078], BF16)
        vTc = m.alloc([3, 286], BF16)
        mk = m.mark()
        S.dma(self.rowsB, io["rowsB"], w=["rowsB"])
        S.dma(self.masks, tabs["masks"], w=["masks"])
        S.dma(self.fmv, io["fmv"], w=["fmv"])
        AB = self.AB
        lt = self.lamt
        if not skip_setup:
            rowsA = m.alloc([4096], F32)
            S.dma(rowsA, io["rowsA"], w=["rowsA"])
            bmodfm = m.alloc([48], F32)
            S.dma(bmodfm, io["bmodfm"], w=["bmodfm"])
            sc = m.alloc([8, 2], F32)
            S.dma(sc, io["sc2"], w=["sc"])
            self.ACT(sc, sc, AF.Silu, ["sc"], ["sc"])
            screp = [m.alloc([8, 128], F32) for _ in range(2)]
            ones_t = m.alloc([128], F32)
            S.dve(lambda e: e.memset(ones_t, 1.0), [], ["ones_t"])
            for j in range(2):
                for c in range(8):
                    self.TS("dve", screp[j][:, c, :], ones_t, sc[:, c, j:j + 1], None, ALU.mult, None, ["sc", "ones_t"], [f"screp{j}"])
            if stop_after == "s1":
                return
            wmr = Ring(m, 2, [8, 512], F32, "wm")
            growr = Ring(m, 2, [512], F32, "grow")
            gtmp = m.alloc([512], F32)
            pg = PRing(pb, [1, 2])
            wmod_v = io["wmod"].rearrange("(c p) n -> p c n", p=128)
            modfm = m.alloc([48, 2], F32)
            for cg in range(12):
                wm, wk = wmr.next()
                S.dma(wm, wmod_v[:, :, cg * 512:(cg + 1) * 512], w=[wk])
                mi = cg // 2
                if mi in (2, 5):
                    for j in range(2):
                        ps, pk = pg.next()
                        for c in range(8):
                            self.MM(ps, screp[j][:, c, :], wm[:, c, :], c == 0, c == 7, [f"screp{j}", wk], [pk])
                        ro = (2048 if mi == 2 else 3072) + (cg % 2) * 512
                        go = (0 if mi == 2 else 1024) + (cg % 2) * 512
                        self.TT("dve", gtmp, ps, rowsA[:, ro:ro + 512], ALU.add, [pk, "rowsA"], ["gtmp"])
                        grow, gk = growr.next()
                        self.TT("pool", grow, gtmp, rowsA[:, go:go + 512], ALU.mult, ["gtmp", "rowsA"], [gk])
                        gi = (0 if mi == 2 else 2) + j
                        S.dma(scr["gates"][gi, :, (cg % 2) * 512:(cg % 2) * 512 + 512], grow, r=[gk], w=[f"gates{gi}"])
                else:
                    for fcc in range(4):
                        idx = cg * 4 + fcc
                        for c in range(8):
                            self.MM(pb[0][:, idx * 2:idx * 2 + 2], wm[:, c, fcc * 128:(fcc + 1) * 128], sc[:, c, :],
                                    c == 0, c == 7, [wk, "sc"], ["pb0"])
            if stop_after == "s2":
                return
            pmod = pb[0][:, 0:96].rearrange("p (a b) -> p a b", b=2)
            for cs in ((0, 16), (24, 40)):
                a, b = cs
                for j in range(2):
                    self.TT("dve", modfm[:, a:b, j], pmod[:, a:b, j], bmodfm[:, a:b], ALU.add, ["pb0", "bmodfm"], ["modfm"])
            if stop_after == "s3":
                return
            AB = self.AB
            gpre_mix = self.fmv[:, 0:8]
            gpre_mlp = self.fmv[:, 8:16]
            for j in range(2):
                self.STT("dve", AB[:, j, 0, :], modfm[:, 8:16, j], 1.0, gpre_mix, ALU.add, ALU.mult, ["modfm", "fmv"], ["AB"])
                self.CP("dve", AB[:, j, 1, :], modfm[:, 0:8, j], ["modfm"], ["AB"])
                self.STT("dve", AB[:, j, 2, :], modfm[:, 32:40, j], 1.0, gpre_mlp, ALU.add, ALU.mult, ["modfm", "fmv"], ["AB"])
                self.CP("dve", AB[:, j, 3, :], modfm[:, 24:32, j], ["modfm"], ["AB"])
            if stop_after == "s4":
                return
            lamrow = self.rowsB[:, 1408:1664].rearrange("p (a b) -> p a b", a=4)
            lt = self.lamt
            lp = m.alloc([2, 64], F32)
            self.TT("dve", lp[:, 0, :], lamrow[:, 0, :], lamrow[:, 1, :], ALU.mult, ["rowsB"], ["lp"])
            self.TT("dve", lp[:, 1, :], lamrow[:, 2, :], lamrow[:, 3, :], ALU.mult, ["rowsB"], ["lp"])
            S.dve(lambda e: e.tensor_reduce(out=lt[:, 0:2], in_=lp, axis=AX.X, op=ALU.add), ["lp"], ["lamt"])
            if stop_after == "s45":
                return
            self.ACT(lt[:, 2:4], lt[:, 0:2], AF.Exp, ["lamt"], ["lamt"])
            self.TT("dve", lt[:, 4:5], lt[:, 2:3], lt[:, 3:4], ALU.subtract, ["lamt"], ["lamt"])
            self.TS("dve", lt[:, 5:6], lt[:, 4:5], -1.0, -lam_init, ALU.mult, ALU.add, ["lamt"], ["lamt"])
            self.TS("dve", self.sublnS, self.rowsB[:, 128:256], 1.0 - lam_init, None, ALU.mult, None, ["rowsB"], ["sublnS"])
        lamneg = lt[:, 5:6]
        if stop_after == "s5":
            return
        qn_row = self.rowsB[:, 0:64]
        kn_row = self.rowsB[:, 64:128]
        cbias_row = self.rowsB[:, 256:640]
        clng_row = self.rowsB[:, 640:1024]
        clnb_row = self.rowsB[:, 1024:1408]
        self.barrier()
        m.release(mk)

        def norm_mod_tile(xt, xk, abi, j, hdst, hk, rings):
            st, sk = rings["st"].next()
            junk, jk = rings["junk"].next()
            self.ACT(junk, xt, AF.Square, [xk], [jk, sk], accum=st[:, 0:1])
            self.rstd(st[:, 0:1], 1024.0, st[:, 2:3], st[:, 1:2], [sk], [sk])
            xn, nk = rings["xn"].next()
            self.TS("dve", xn, xt, st[:, 2:3], None, ALU.mult, None, [xk, sk], [nk])
            pT = pbf(0).rearrange("p (a b) -> p a b", a=8)
            for c in range(8):
                self.TR(pT[:, c, :], xn[:, c * 128:(c + 1) * 128], [nk], ["pb0"])
            for c in range(8):
                self.ACT(hdst[:, c, :], pT[:, c, :], AF.Identity, ["pb0", "AB"], [hk],
                         scale=AB[:, j, 2 * abi, c:c + 1], bias=AB[:, j, 2 * abi + 1, c:c + 1])

        def qk_post(ps, pk, H, normrow, rope_t, out_bf, ok, rings):
            xf, xk = rings["xf"].next()
            xf = xf[:, 0:H * 64]
            self.CP("act", xf, ps, [pk], [xk])
            x3 = xf.rearrange("p (h d) -> p h d", h=H)
            if normrow is not None:
                sq, qk = rings["xf"].next()
                st, sk = rings["st"].next()
                for h in range(H):
                    self.ACT(sq[:, 0:64], x3[:, h, :], AF.Square, [xk], [qk, sk], accum=st[:, h:h + 1])
                self.rstd(st[:, 0:H], 64.0, st[:, 16:16 + H], st[:, 8:8 + H], [sk], [sk])
                for h in range(H):
                    self.STT("dve", x3[:, h, :], x3[:, h, :], st[:, 16 + h:17 + h], normrow, ALU.mult, ALU.mult,
                             [xk, sk, "rowsB"], [xk])
            if rope_t is not None:
                x5 = xf.rearrange("p (h a r f) -> p h a r f", h=H, a=2, r=2)
                o5 = out_bf.rearrange("p h (a r f) -> p h a r f", a=2, r=2)
                re, im = x5[:, :, :, 0, :], x5[:, :, :, 1, :]
                cs = self.cosT[:, rope_t, :].rearrange("p (a f) -> p a f", a=2).unsqueeze(1).broadcast_to([128, H, 2, 16])
                sn = self.sinT[:, rope_t, :].rearrange("p (a f) -> p a f", a=2).unsqueeze(1).broadcast_to([128, H, 2, 16])
                t1, k1 = rings["rt"].next()
                t2, k2 = rings["rt"].next()
                t1 = t1[:, 0:H * 32].rearrange("p (h a f) -> p h a f", h=H, a=2)
                t2 = t2[:, 0:H * 32].rearrange("p (h a f) -> p h a f", h=H, a=2)
                self.TT("dve", t1, re, cs, ALU.mult, [xk, "cosT"], [k1])
                self.TT("pool", t2, im, sn, ALU.mult, [xk, "sinT"], [k2])
                self.TT("dve", o5[:, :, :, 0, :], t1, t2, ALU.subtract, [k1, k2], [ok])
                t3, k3 = rings["rt"].next()
                t4, k4 = rings["rt"].next()
                t3 = t3[:, 0:H * 32].rearrange("p (h a f) -> p h a f", h=H, a=2)
                t4 = t4[:, 0:H * 32].rearrange("p (h a f) -> p h a f", h=H, a=2)
                self.TT("dve", t3, im, cs, ALU.mult, [xk, "cosT"], [k3])
                self.TT("pool", t4, re, sn, ALU.mult, [xk, "sinT"], [k4])
                self.TT("pool", o5[:, :, :, 1, :], t3, t4, ALU.add, [k3, k4], [ok])
            else:
                self.CP("dve", out_bf, x3, [xk], [ok])

        if stop_after == "setup":
            return
        mk = m.mark()
        WB = m.alloc([8, 1664], BF16)
        win_v = io["w_in"].rearrange("(c p) n -> p c n", p=128)
        if reuse_kv:
            WB_loaded = False
        else:
            WB_loaded = True
        self.cosT = m.alloc([32, 32], F32)
        self.sinT = m.alloc([32, 32], F32)
        S.dma(self.cosT, tabs["cos"], w=["cosT"])
        S.dma(self.sinT, tabs["sin"], w=["sinT"])
        for c in range(8 if WB_loaded else 0):
            S.dma(WB[:, c, :], win_v[:, c, 0:1664], w=["WB"], q="pool")
        rings = {
            "st": Ring(m, 4, [32], F32, "st"),
            "junk": Ring(m, 1, [1024], BF16, "junk"),
            "xn": Ring(m, 2, [1024], BF16, "xn"),
            "xf": Ring(m, 4, [512], F32, "xf"),
            "rt": Ring(m, 8, [256], F32, "rt"),
        }
        xr = Ring(m, 2, [1024], F32, "xt")
        hTtmp = Ring(m, 2, [8, 128], BF16, "hTt")
        vdr = Ring(m, 2, [4, 129], BF16, "vdst")
        vgr = Ring(m, 2, [2, 65], BF16, "vgst")
        ustr = Ring(m, 2, [384], BF16, "ust")
        kbr = Ring(m, 2, [512], BF16, "kb")
        kdr = Ring(m, 2, [2, 2, 64], BF16, "kdup")
        ktdr = Ring(m, 2, [4, 128], BF16, "ktdst")
        ktgr = Ring(m, 2, [2, 128], BF16, "ktgst")
        for b_ in vdr.bufs:
            S.dve(lambda e, b_=b_: e.memset(b_[:, :, 128:129], 1.0), [], ["vdst0", "vdst1"])
        for b_ in vgr.bufs:
            S.dve(lambda e, b_=b_: e.memset(b_[:, :, 64:65], 1.0), [], ["vgst0", "vgst1"])
        pr = PRing(pb, [2, 3, 4, 5, 6, 7])
        xr = Ring(m, 3, [1024], F32, "xt3")
        tiles_b1 = [t for t in range(NT_ALL) if not (reuse_kv and (t >= 32 or (t >= 16 and t not in (16, 31))))]
        xloads = {}

        def issue_xload(i):
            if i < len(tiles_b1) and i not in xloads:
                tt_ = tiles_b1[i]
                xt_, xk_ = xr.next()
                if gather_from is not None and 16 <= tt_ < 32:
                    idx_ap = self.idx_s[:, tt_:tt_ + 1]
                    S.op("pool", lambda e, xt_=xt_, idx_ap=idx_ap: e.indirect_dma_start(
                        out=xt_, out_offset=None, in_=gather_from[:, :],
                        in_offset=bass.IndirectOffsetOnAxis(ap=idx_ap, axis=0)), ["idx_s", "SHXready"], [xk_], dma=True)
                else:
                    S.dma(xt_, x_all[tt_ * 128:(tt_ + 1) * 128, :], w=[xk_])
                xloads[i] = (xt_, xk_)
        issue_xload(0)
        issue_xload(1)
        for ti_, t in enumerate(tiles_b1):
            kind = "own" if t < 16 else ("other" if t < 32 else "ctx")
            issue_xload(ti_ + 2)
            xt, xk = xloads[ti_]
            if kind == "own":
                hdst, hk = hT[:, :, t * 128:(t + 1) * 128], f"hT{t}"
            elif kind == "ctx":
                sl = 16 + (t - 32)
                hdst, hk = hT[:, :, sl * 128:(sl + 1) * 128], f"hT{sl}"
            elif t in (16, 31):
                hi = 0 if t == 16 else 1
                hdst, hk = hTh[:, :, hi * 128:(hi + 1) * 128], f"hTh{hi}"
            else:
                hdst, hk = hTtmp.next()
            j = 1 if kind == "ctx" else 0
            norm_mod_tile(xt, xk, 0, j, hdst, hk, rings)
            if reuse_kv:
                continue
            rope_t = None if kind == "ctx" else t

            def proj(c0, c1):
                ps, pk = pr.next()
                for c in range(8):
                    self.MM(ps[:, 0:c1 - c0], hdst[:, c, :], WB[:, c, c0:c1], c == 0, c == 7, [hk, "WB"], [pk])
                return ps, pk
            ps, pk = proj(0, 512)
            kb, kk = kbr.next()
            qk_post(ps, pk, 8, None, rope_t, kb.rearrange("p (h d) -> p h d", h=8), kk, rings)
            pT = pbf(1)[:, 0:512].rearrange("p (a b) -> p a b", a=4)
            for h in range(4):
                self.TR(pT[:, h, :], kb[:, h * 128:(h + 1) * 128], [kk], ["pb1"])
            ks, ksk = ktdr.next()
            self.CP("act", ks, pT, ["pb1"], [ksk])
            S.dma(scr["KTd"][:, :, t * 128:(t + 1) * 128].rearrange("h p k -> p h k"), ks, r=[ksk], w=["KTd"])
            ps, pk = proj(512, 1024)
            vs, vk = vdr.next()
            self.CP("act", vs[:, :, 0:128], ps.rearrange("p (h d) -> p h d", h=4), [pk], [vk])
            S.dma(scr["Vd"][:, :, t, :].rearrange("h p c -> p h c"), vs, r=[vk], w=["Vd"])
            ps, pk = proj(1024, 1280)
            vs, vk = vgr.next()
            self.CP("act", vs[:, :, 0:64], ps[:, 128:256].rearrange("p (h d) -> p h d", h=2), [pk], [vk])
            S.dma(scr["Vg"][:, :, t, :].rearrange("h p c -> p h c"), vs, r=[vk], w=["Vg"])
            kd, kdk = kdr.next()
            qk_post(ps[:, 0:128], pk, 2, kn_row, rope_t, kd[:, :, 0, :], kdk, rings)
            self.CP("pool", kd[:, :, 1, :], kd[:, :, 0, :], [kdk], [kdk])
            pT2 = pbf(1)[:, 512:768].rearrange("p (a b) -> p a b", a=2)
            for h in range(2):
                self.TR(pT2[:, h, :], kd[:, h, :, :].rearrange("p a b -> p (a b)"), [kdk], ["pb1"])
            ks, ksk = ktgr.next()
            self.CP("act", ks, pT2, ["pb1"], [ksk])
            S.dma(scr["KTg"][:, :, t * 128:(t + 1) * 128].rearrange("h p k -> p h k"), ks, r=[ksk], w=["KTg"])
            if kind != "ctx" or ctx_full:
                ps, pk = proj(1280, 1664)
                us, uk = ustr.next()
                self.CP("dve", us, ps[:, 0:384], [pk], [uk])
                S.dma(scr["U"][:, t, :], us, r=[uk], w=["U"])
        self.barrier()
        m.release(mk)

        if stop_after == "B1":
            return
        mk = m.mark()
        mkB2 = mk
        QTd = m.alloc([4, 2304], BF16)
        QTg = m.alloc([4, 2304], BF16)
        OdT = m.alloc([4, 2304], BF16)
        OgT = m.alloc([4, 2304], BF16)
        mk_after_res = m.mark()
        W2B = m.alloc([8, 1792], BF16)
        self.cosT = m.alloc([32, 32], F32)
        self.sinT = m.alloc([32, 32], F32)
        S.dma(self.cosT, tabs["cos"], w=["cosT"])
        S.dma(self.sinT, tabs["sin"], w=["sinT"])
        for c in range(8):
            S.dma(W2B[:, c, :], win_v[:, c, 1664:3456], w=["W2B"], q="pool")
        rings = {
            "st": Ring(m, 4, [32], F32, "st"),
            "xf": Ring(m, 4, [512], F32, "xf"),
            "rt": Ring(m, 8, [256], F32, "rt"),
        }
        qbr = Ring(m, 2, [512], BF16, "qb")
        sgr = Ring(m, 2, [3, 128], F32, "sg")
        S.dve(lambda e: e.memset(vTc, 0.0), [], ["vTc"])
        pr = PRing(pb, [2, 3, 4, 5, 6, 7])

        def glu(hsrc, hk):
            pa, pak = pr.next()
            pg_, pgk = pr.next()
            pa3 = pa[:, 0:384].rearrange("p (a b) -> p a b", a=3)
            pg3 = pg_[:, 0:384].rearrange("p (a b) -> p a b", a=3)
            for cc in range(3):
                for c in range(8):
                    self.MM(pa3[:, cc, :], W2B[:, c, 1024 + cc * 128:1024 + (cc + 1) * 128], hsrc[:, c, :],
                            c == 0, c == 7, ["W2B", hk], [pak])
            for cc in range(3):
                for c in range(8):
                    self.MM(pg3[:, cc, :], W2B[:, c, 1408 + cc * 128:1408 + (cc + 1) * 128], hsrc[:, c, :],
                            c == 0, c == 7, ["W2B", hk], [pgk])
            sg, sgk = sgr.next()
            self.ACT(sg, pg3, AF.Sigmoid, [pgk], [sgk])
            return pa3, pak, sg, sgk

        for s in range(n_slots):
            hsrc, hk = hT[:, :, s * 128:(s + 1) * 128], f"hT{s}"
            rope_t = s if s < 16 else None
            for (c0, norm, QT, qkey) in ((0, None, QTd, "QTd"), (512, qn_row, QTg, "QTg")):
                ps, pk = pr.next()
                for c in range(8):
                    self.MM(ps, hsrc[:, c, :], W2B[:, c, c0:c0 + 512], c == 0, c == 7, [hk, "W2B"], [pk])
                qb, qk_ = qbr.next()
                qk_post(ps, pk, 8, norm, rope_t, qb.rearrange("p (h d) -> p h d", h=8), qk_, rings)
                pT = pbf(1)[:, 0:512].rearrange("p (a b) -> p a b", a=4)
                for h in range(4):
                    self.TR(pT[:, h, :], qb[:, h * 128:(h + 1) * 128], [qk_], ["pb1"])
                self.CP("act", QT[:, :, s * 128:(s + 1) * 128], pT, ["pb1"], [f"{qkey}{s}"])
            pa3, pak, sg, sgk = glu(hsrc, hk)
            if s < 16:
                self.TT("dve", vT[:, :, 15 + s * 128:15 + (s + 1) * 128], pa3, sg, ALU.mult, [pak, sgk], ["vT"])
            else:
                o = 15 + (s - 16) * 128
                self.TT("dve", vTc[:, :, o:o + 128], pa3, sg, ALU.mult, [pak, sgk], ["vTc"])
        for hi in range(2):
            hsrc, hk = hTh[:, :, hi * 128:(hi + 1) * 128], f"hTh{hi}"
            pa3, pak, sg, sgk = glu(hsrc, hk)
            tmp, tk = sgr.next()
            self.TT("dve", tmp, pa3, sg, ALU.mult, [pak, sgk], [tk])
            if hi == 0:
                self.TS("dve", vT[:, :, 2063:2078], tmp[:, :, 0:15], self.masks[:, 1:2], None, ALU.mult, None, [tk, "masks"], ["vT"])
            else:
                self.TS("dve", vT[:, :, 0:15], tmp[:, :, 113:128], self.masks[:, 0:1], None, ALU.mult, None, [tk, "masks"], ["vT"])
        self.barrier()
        self.dump("hT", hT); self.dump("QTd", QTd); self.dump("QTg", QTg); self.dump("vT", vT); self.dump("vTc", vTc)
        self.dump("AB", self.AB); self.dump("lamt", self.lamt)
        m.release(mk_after_res)

        if stop_after == "B2":
            return
        mkC = m.mark()
        ktr = Ring(m, 2, [NKEY], BF16, "KT")
        vr = Ring(m, 2, [34 * 129], BF16, "V")
        ptr_ = Ring(m, 3, [512], BF16, "PT")
        afr = Ring(m, 2, [2, 2, 129], F32, "af")
        obr = Ring(m, 2, [2, 128], BF16, "ob")
        o32r = Ring(m, 2, [128], F32, "o32")
        ttr = Ring(m, 2, [128], F32, "tt")
        str_ = Ring(m, 4, [16], F32, "sta")
        sjunk = m.alloc([128], BF16)
        ps_s = self.pp[0].rearrange("p (j b q) -> p j b q", j=2, b=2)
        s_cnt = [0]
        accb = [[pb[4], pb[5]], [pb[6], pb[7]]]
        acck = [["pb4", "pb5"], ["pb6", "pb7"]]
        obhr = Ring(m, 2, [18, 128], BF16, "obh")
        qblocks = [(i * 256, list(range(34))) for i in range(8)]
        if ctx_full:
            qblocks.append((2048, [32, 33]))
        import os
        _cu = [int(x) for x in os.environ.get("CUNITS", "0,1,2,3,4,5,6,7").split(",")]
        _cqb = int(os.environ.get("CQB", "99"))
        _cpost = os.environ.get("CNOPOST", "") == ""
        qblocks = qblocks[:_cqb]
        for ui in _cu:
            diff = ui < 4
            idx = ui % 4
            dv = 128 if diff else 64
            kt, ktk = ktr.next()
            vb, vk = vr.next()
            if diff:
                S.dma(kt, scr["KTd"][idx], r=["KTd"], w=[ktk])
                v3 = vb.rearrange("p (t c) -> p t c", t=34)
                S.dma(v3, scr["Vd"][idx], r=["Vd"], w=[vk])
                QT, qkey, OT, okey = QTd, "QTd", OdT, "OdT"
            else:
                kvh = idx // 2
                S.dma(kt, scr["KTg"][kvh], r=["KTg"], w=[ktk])
                v3 = vb[:, 0:34 * 65].rearrange("p (t c) -> p t c", t=34)
                S.dma(v3, scr["Vg"][kvh], r=["Vg"], w=[vk])
                QT, qkey, OT, okey = QTg, "QTg", OgT, "OgT"
            units = [(q0, kcs, ki) for (q0, kcs) in qblocks for ki in range(len(kcs))]
            pending = []
            obh, obhk = obhr.next()

            def emit_S(u):
                q0, kcs, ki = units[u]
                kc = kcs[ki]
                bsel = s_cnt[0] % 2
                s_cnt[0] += 1
                sb, sk = self.pp[bsel].rearrange("p (j q) -> p j q", j=2)[:, :, 0:256], f"ps_s{bsel}"
                qkeys = [f"{qkey}{q0 // 128}", f"{qkey}{q0 // 128 + 1}"]
                for j in range(int(os.environ.get("CJ", "2"))):
                    self.MM(sb[:, j, :], kt[j * 64:(j + 1) * 64, kc * 128:(kc + 1) * 128],
                            QT[j * 64:(j + 1) * 64, idx, q0:q0 + 256], True, True, [ktk] + qkeys, [sk])
                return sb, sk

            _cstage = int(os.environ.get("CSTAGE", "9"))
            if _cstage == 0:
                continue
            nxt = emit_S(0)
            for u in range(len(units)):
                q0, kcs, ki = units[u]
                kc = kcs[ki]
                sb, sk = nxt
                if u + 1 < len(units):
                    nxt = emit_S(u + 1)
                pt, pk_ = ptr_.next()
                if os.environ.get("CNOEXP", "") == "":
                    self.ACT(pt.rearrange("p (j q) -> p j q", j=2), sb, AF.Exp, [sk], [pk_], scale=0.125)
                for j in range(2 if _cstage >= 2 else 0):
                    for sub in range(2):
                        self.MM(accb[j][sub][:, 0:dv + 1], pt[:, j * 256 + sub * 128:j * 256 + (sub + 1) * 128],
                                v3[:, kc, :], ki == 0, ki == len(kcs) - 1, [pk_, vk], [acck[j][sub]])
                while pending and pending[0][0] <= u:
                    pending.pop(0)[1]()
                if ki == len(kcs) - 1 and _cpost:
                    af, afk = afr.next()
                    for j in range(2):
                        for sub in range(2):
                            self.CP("act" if (j + sub) % 2 == 0 else "dve", af[:, j, sub, 0:dv + 1],
                                    accb[j][sub][:, 0:dv + 1], [acck[j][sub]], [afk])
                    st, stk = str_.next()
                    self.RECIP(st[:, 0:4].rearrange("p (a b) -> p a b", a=2), af[:, :, :, dv], [afk], [stk])
                    ob, obk = obh[:, q0 // 128:q0 // 128 + 2, :], obhk
                    if diff:
                        self.TS("dve", st[:, 4:6], st[:, 2:4], lamneg, None, ALU.mult, None, [stk, "lamt"], [stk])
                        for sub in range(2):
                            tt, ttk = ttr.next()
                            o32, o32k = o32r.next()
                            self.TS("pool", tt, af[:, 1, sub, 0:128], st[:, 4 + sub:5 + sub], None, ALU.mult, None, [afk, stk], [ttk])
                            self.STT("dve", o32, af[:, 0, sub, 0:128], st[:, sub:sub + 1], tt, ALU.mult, ALU.add, [afk, stk, ttk], [o32k])
                            self.ACT(sjunk, o32, AF.Square, [o32k], ["sjunk", stk], accum=st[:, 8 + sub:9 + sub])
                            self.rstd(st[:, 8 + sub:9 + sub], 128.0, st[:, 12 + sub:13 + sub], st[:, 10 + sub:11 + sub], [stk], [stk])
                            self.STT("dve", ob[:, sub, :], o32, st[:, 12 + sub:13 + sub], self.sublnS, ALU.mult, ALU.mult,
                                     [o32k, stk, "sublnS"], [obk])
                    else:
                        for sub in range(2):
                            for j in range(2):
                                self.TS("dve" if j == 0 else "pool", ob[:, sub, j * 64:(j + 1) * 64], af[:, j, sub, 0:64],
                                        st[:, j * 2 + sub:j * 2 + sub + 1], None, ALU.mult, None, [afk, stk], [obk])

            nqt = len(qblocks) * 2
            for g0 in range(0, nqt, 8):
                gn = min(8, nqt - g0)
                bk = 4 + g0 // 8
                pT = pbf(bk)[:, 0:gn * 128].rearrange("p (a b) -> p a b", a=gn)
                for qi in range(gn):
                    self.TR(pT[:, qi, :], obh[:, g0 + qi, :], [obhk], [f"pb{bk}"])
                self.CP("act" if (g0 // 8) % 2 == 0 else "dve", OT[:, idx, g0 * 128:(g0 + gn) * 128], pbf(bk)[:, 0:gn * 128], [f"pb{bk}"],
                        [f"{okey}{g0 + qi}" for qi in range(gn)])
        self.barrier()
        self.dump("OdT", OdT); self.dump("OgT", OgT)
        m.release(mkC)
        if stop_after == "C":
            return
        YfT = QTd.rearrange("p a b -> p (a b)")[:, 0:3 * 2304].rearrange("p (a b) -> p a b", a=3)
        YcT = QTg.rearrange("p a b -> p (a b)")[:, 0:3 * 2304].rearrange("p (a b) -> p a b", a=3)

        mkD = m.mark()
        Usb = m.alloc([34, 384], BF16)
        S.dma(Usb, scr["U"], r=["U"], w=["Usb"])
        ctab = Ring(m, 2, [4, 512], BF16, "ctab")
        stab = Ring(m, 2, [4, 512], BF16, "stab")
        pqr = Ring(m, 6, [512], BF16, "pq")
        dc_v = tabs["dftc"].rearrange("(c p) n -> p c n", p=128)
        ds_v = tabs["dfts"].rearrange("(c p) n -> p c n", p=128)
        for lb in range(4):
            for lcg in range(8):
                ct, ck = ctab.next()
                stt, sk_ = stab.next()
                S.dma(ct, dc_v[:, lcg * 4:(lcg + 1) * 4, lb * 512:(lb + 1) * 512], w=[ck])
                S.dma(stt, ds_v[:, lcg * 4:(lcg + 1) * 4, lb * 512:(lb + 1) * 512], w=[sk_])
                for cc in range(3):
                    for li in range(4):
                        lc = lcg * 4 + li
                        self.MM(pb[cc], Usb[:, lc, cc * 128:(cc + 1) * 128], ct[:, li, :], lc == 0, lc == 31, ["Usb", ck], [f"pb{cc}"])
                        self.MM(pb[3 + cc], Usb[:, lc, cc * 128:(cc + 1) * 128], stt[:, li, :], lc == 0, lc == 31, ["Usb", sk_], [f"pb{3 + cc}"])
            for cc in range(3):
                pq, pqk = pqr.next()
                self.CP("act", pq, pb[cc], [f"pb{cc}"], [pqk])
                qq, qqk = pqr.next()
                self.CP("dve", qq, pb[3 + cc], [f"pb{3 + cc}"], [qqk])
                self.MM(pb[6], self.cbd, pq, True, False, ["cbd", pqk], ["pb6"])
                self.MM(pb[6], self.nsbd, qq, False, True, ["nsbd", qqk], ["pb6"])
                self.CP("act", YfT[:, cc, lb * 512:(lb + 1) * 512], pb[6], ["pb6"], [f"YfT{lb}"])
        if ctx_full:
            cct = m.alloc([2, 256], BF16)
            sct = m.alloc([2, 256], BF16)
            S.dma(cct, io["dftcc"].rearrange("(c p) n -> p c n", p=128), w=["cct"])
            S.dma(sct, io["dftsc"].rearrange("(c p) n -> p c n", p=128), w=["sct"])
            for cc in range(3):
                for lc in range(2):
                    self.MM(pb[0][:, 0:256], Usb[:, 32 + lc, cc * 128:(cc + 1) * 128], cct[:, lc, :], lc == 0, lc == 1, ["Usb", "cct"], ["pb0"])
                    self.MM(pb[1][:, 0:256], Usb[:, 32 + lc, cc * 128:(cc + 1) * 128], sct[:, lc, :], lc == 0, lc == 1, ["Usb", "sct"], ["pb1"])
                pq, pqk = pqr.next()
                self.CP("act", pq[:, 0:256], pb[0][:, 0:256], ["pb0"], [pqk])
                qq, qqk = pqr.next()
                self.CP("dve", qq[:, 0:256], pb[1][:, 0:256], ["pb1"], [qqk])
                self.MM(pb[6][:, 0:256], self.cbd, pq[:, 0:256], True, False, ["cbd", pqk], ["pb6"])
                self.MM(pb[6][:, 0:256], self.nsbd, qq[:, 0:256], False, True, ["nsbd", qqk], ["pb6"])
                self.CP("act", YfT[:, cc, 2048:2304], pb[6][:, 0:256], ["pb6"], ["YfT4"])
        self.barrier()
        self.dump("YfT", YfT)
        m.release(mkD)

        if stop_after == "D":
            return
        mkE = m.mark()
        Dm = m.alloc([93, 128], BF16)
        identf = m.alloc([128], F32)
        self.CP("dve", identf, self.ident, ["ident"], ["identf"])
        dwfm = self.fmv[:, 16:109].rearrange("p (a b) -> p a b", a=3)
        for cc in range(3):
            for k in range(31):
                self.TS("dve" if k % 2 == 0 else "pool", Dm[:, cc * 31 + k, :], identf, dwfm[:, cc, k:k + 1], None, ALU.mult, None,
                        ["identf", "fmv"], ["Dm"])
        ybr = Ring(m, 2, [384], F32, "yb")
        ycr = Ring(m, 2, [384], F32, "yc")
        ysr = Ring(m, 2, [384], BF16, "ys")
        st5 = Ring(m, 4, [16], F32, "st5")
        cjunk = m.alloc([384], BF16)
        pr = PRing(pb, [0, 1, 2, 3])
        for s in range(n_slots):
            if s < 16:
                src, base, sk_ = vT, s * 128, "vT"
            else:
                src, base, sk_ = vTc, (s - 16) * 128, "vTc"
            ps, pk = pr.next()
            for cc in range(3):
                for k in range(31):
                    self.MM(ps[:, cc * 128:(cc + 1) * 128], src[:, cc, base + k:base + k + 128], Dm[:, cc * 31 + k, :],
                            k == 0, k == 30, [sk_, "Dm"], [pk])
            yb, ybk = ybr.next()
            st, stk = st5.next()
            self.TT("dve", yb, ps[:, 0:384], cbias_row, ALU.add, [pk, "rowsB"], [ybk])
            S.dve(lambda e, st=st, yb=yb: e.tensor_reduce(out=st[:, 0:1], in_=yb, axis=AX.X, op=ALU.add), [ybk], [stk])
            self.TS("dve", st[:, 1:2], st[:, 0:1], -1.0 / 384, None, ALU.mult, None, [stk], [stk])
            yc, yck = ycr.next()
            self.TS("pool", yc, yb, st[:, 1:2], None, ALU.add, None, [ybk, stk], [yck])
            self.ACT(cjunk, yc, AF.Square, [yck], ["cjunk", stk], accum=st[:, 2:3])
            self.rstd(st[:, 2:3], 384.0, st[:, 4:5], st[:, 3:4], [stk], [stk])
            self.STT("dve", yb, yc, st[:, 4:5], clng_row, ALU.mult, ALU.mult, [yck, stk, "rowsB"], [ybk])
            self.TT("pool", yc, yb, clnb_row, ALU.add, [ybk, "rowsB"], [yck])
            ys, ysk = ysr.next()
            self.ACT(ys, yc, AF.Silu, [yck], [ysk])
            pT = pbf(4)[:, 0:384].rearrange("p (a b) -> p a b", a=3)
            for cc in range(3):
                self.TR(pT[:, cc, :], ys[:, cc * 128:(cc + 1) * 128], [ysk], ["pb4"])
            self.CP("act", YcT[:, :, s * 128:(s + 1) * 128], pT, ["pb4"], [f"YcT{s}"])
        self.barrier()
        self.dump("YcT", YcT)
        m.release(mkE)

        if stop_after == "E":
            return
        mkF = m.mark()
        mT = m.alloc([8, 2304], BF16)
        mkFw = m.mark()
        wgr = Ring(m, 2, [8, 4, 128], BF16, "wg")
        wbr = Ring(m, 2, [14, 128], BF16, "wb")
        sgr = Ring(m, 2, [512], F32, "sgf")
        tmr = Ring(m, 2, [512], F32, "tmf")
        macc = m.alloc([512], F32)
        gr = PRing(pb, [0, 1, 2, 3])
        br_ = PRing(pb, [4, 5, 6, 7])
        brw = [("w_brf", 3, YfT, "YfT"), ("w_brd", 4, OdT, "OdT"), ("w_brg", 4, OgT, "OgT"), ("w_brc", 3, YcT, "YcT")]
        tblocks = [(i * 512, 512) for i in range(4)] + ([(2048, 256)] if ctx_full else [])
        for fc in range(8):
            wg, wgk = wgr.next()
            wb, wbk = wbr.next()
            for bi in range(4):
                c0 = GATE_COL0 + bi * 1024 + fc * 128
                S.dma(wg[:, :, bi, :], win_v[:, :, c0:c0 + 128], w=[wgk], q="pool")
            ko = 0
            for (nm, nk_, _, _) in brw:
                S.dma(wb[:, ko:ko + nk_, :], io[nm].rearrange("(c p) n -> p c n", p=128)[:, :, fc * 128:(fc + 1) * 128],
                      w=[wbk], q="pool")
                ko += nk_
            for (t0, tn) in tblocks:
                hkeys = [f"hT{t0 // 128 + i}" for i in range(tn // 128)]
                ko = 0
                for bi, (nm, nk_, Y, ykey) in enumerate(brw):
                    pg_, pgk = gr.next()
                    for c in range(8):
                        self.MM(pg_[:, 0:tn], wg[:, c, bi, :], hT[:, c, t0:t0 + tn], c == 0, c == 7, [wgk] + hkeys, [pgk])
                    pbr, pbk = br_.next()
                    if ykey in ("OdT", "OgT", "YcT"):
                        ykeys = [f"{ykey}{t0 // 128 + i}" for i in range(tn // 128)]
                    else:
                        ykeys = [f"YfT{t0 // 512}"]
                    for kc in range(nk_):
                        self.MM(pbr[:, 0:tn], wb[:, ko + kc, :], Y[:, kc, t0:t0 + tn], kc == 0, kc == nk_ - 1, [wbk] + ykeys, [pbk])
                    ko += nk_
                    sg, sgk = sgr.next()
                    self.ACT(sg[:, 0:tn], pg_[:, 0:tn], AF.Sigmoid, [pgk], [sgk])
                    if bi == 0:
                        self.TT("dve", macc[:, 0:tn], pbr[:, 0:tn], sg[:, 0:tn], ALU.mult, [pbk, sgk], ["macc"])
                    else:
                        tm, tmk = tmr.next()
                        self.TT("dve", tm[:, 0:tn], pbr[:, 0:tn], sg[:, 0:tn], ALU.mult, [pbk, sgk], [tmk])
                        if bi < 3:
                            self.TT("pool", macc[:, 0:tn], macc[:, 0:tn], tm[:, 0:tn], ALU.add, ["macc", tmk], ["macc"])
                        else:
                            self.TT("pool", mT[:, fc, t0:t0 + tn], macc[:, 0:tn], tm[:, 0:tn], ALU.add, ["macc", tmk],
                                    [f"mT{t0 // 512}"])
        self.barrier()
        self.dump("mT", mT)
        m.release(mkFw)

        if stop_after == "F":
            return
        m_main = m
        m = Mem(self.arena, mkF)
        m.off = mkB2
        Wo = m.alloc([8, 1024], BF16)
        S.dma(Wo, io["w_out"].rearrange("(c p) n -> p c n", p=128), w=["Wo"], q="pool")
        g1 = [m.alloc([1024], F32) for _ in range(2)]
        for j in range(2):
            S.dma(g1[j], scr["gates"][j], r=[f"gates{j}"], w=[f"g1{j}"])
        rings = {
            "st": Ring(m, 4, [32], F32, "st"),
            "junk": Ring(m, 1, [1024], BF16, "junk"),
            "xn": Ring(m, 2, [1024], BF16, "xn"),
        }
        xr = Ring(m, 2, [1024], F32, "xres")
        yr = Ring(m, 2, [1024], F32, "yy")
        xmr = Ring(m, 2, [1024], F32, "xm")
        pr = PRing(pb, [2, 3, 4, 5, 6, 7])

        def post_norm_res(pss, pks, gate, gatek, xres, xresk, xout, xoutk, rings):
            st, stk = rings["st"].next()
            junk, jk = rings["junk"].next()
            for hf in range(2):
                self.ACT(junk[:, 0:512], pss[hf], AF.Square, [pks[hf]], [jk, stk], accum=st[:, 4 + hf:5 + hf])
            self.TT("dve", st[:, 0:1], st[:, 4:5], st[:, 5:6], ALU.add, [stk], [stk])
            self.rstd(st[:, 0:1], 1024.0, st[:, 2:3], st[:, 1:2], [stk], [stk])
            y, yk = yr.next()
            for hf in range(2):
                self.STT("dve", y[:, hf * 512:(hf + 1) * 512], pss[hf], st[:, 2:3], gate[:, hf * 512:(hf + 1) * 512],
                         ALU.mult, ALU.mult, [pks[hf], stk, gatek], [yk])
            self.TT("pool", xout, xres, y, ALU.add, [xresk, yk], [xoutk])

        for s in range(n_slots):
            j = 0 if s < 16 else 1
            row0 = s * 128 if s < 16 else 4096 + (s - 16) * 128
            pss, pks = [], []
            for hf in range(2):
                ps, pk = pr.next()
                for c in range(8):
                    self.MM(ps, mT[:, c, s * 128:(s + 1) * 128], Wo[:, c, hf * 512:(hf + 1) * 512], c == 0, c == 7,
                            [f"mT{s // 4}", "Wo"], [pk])
                pss.append(ps)
                pks.append(pk)
            xres, xresk = xr.next()
            S.dma(xres, x_all[row0:row0 + 128, :], w=[xresk])
            xm, xmk = xmr.next()
            post_norm_res(pss, pks, g1[j], f"g1{j}", xres, xresk, xm, xmk, rings)
            S.dma(scr["xmid"][s * 128:(s + 1) * 128, :], xm, r=[xmk], w=[f"xmid{s}"])
            norm_mod_tile(xm, xmk, 1, j, hT[:, :, s * 128:(s + 1) * 128], f"hT{s}", rings)
        self.barrier()
        m = m_main
        m.release(mk_layer)

        if stop_after == "F2":
            return
        mkG = m.mark()
        w1v = io["w_ff1"]
        for c in range(8):
            S.dma(scr["W1b"][c * 128:(c + 1) * 128, :], w1v[c * 128:(c + 1) * 128, :], w=["W1b"], q="pool")
        W2 = m.alloc([32, 1024], BF16)
        w2v = io["w_ff2"].rearrange("(c p) n -> p c n", p=128)
        for g in range(8):
            S.dma(W2[:, g * 4:(g + 1) * 4, :], w2v[:, g * 4:(g + 1) * 4, :], w=["W2"], q="pool")
        g2 = [m.alloc([1024], F32) for _ in range(2)]
        for j in range(2):
            S.dma(g2[j], scr["gates"][2 + j], r=[f"gates{2 + j}"], w=[f"g2{j}"])
        aT = m.alloc([32, 512], BF16)
        w1r = Ring(m, 3, [8, 512], BF16, "w1")
        rlr = Ring(m, 2, [512], F32, "rl")
        rings = {
            "st": Ring(m, 4, [32], F32, "st"),
            "junk": Ring(m, 1, [1024], BF16, "junk"),
        }
        xr = Ring(m, 2, [1024], F32, "xres")
        yr = Ring(m, 2, [1024], F32, "yy")
        xor_ = Ring(m, 2, [1024], F32, "xo")
        p1 = PRing(pb, [0, 1, 2, 3])
        p2 = PRing(pb, [4, 5, 6, 7])
        W1b_v = scr["W1b"].rearrange("(c p) n -> p c n", p=128)
        for (t0, tn) in tblocks:
            hkeys = [f"hT{t0 // 128 + i}" for i in range(tn // 128)]
            for fg in range(8):
                w1, w1k = w1r.next()
                S.dma(w1, W1b_v[:, :, fg * 512:(fg + 1) * 512], r=["W1b"], w=[w1k])
                for fi in range(4):
                    fch = fg * 4 + fi
                    ps, pk = p1.next()
                    for c in range(8):
                        self.MM(ps[:, 0:tn], w1[:, c, fi * 128:(fi + 1) * 128], hT[:, c, t0:t0 + tn], c == 0, c == 7,
                                [w1k] + hkeys, [pk])
                    rl, rlk = rlr.next()
                    self.ACT(rl[:, 0:tn], ps[:, 0:tn], AF.Relu, [pk], [rlk])
                    self.TT("pool" if fch % 2 == 0 else "dve", aT[:, fch, 0:tn], rl[:, 0:tn], rl[:, 0:tn], ALU.mult, [rlk], ["aT"])
            for ti in range(tn // 128):
                s = t0 // 128 + ti
                j = 0 if s < 16 else 1
                pss, pks = [], []
                for hf in range(2):
                    ps, pk = p2.next()
                    for fch in range(32):
                        self.MM(ps, aT[:, fch, ti * 128:(ti + 1) * 128], W2[:, fch, hf * 512:(hf + 1) * 512], fch == 0, fch == 31,
                                ["aT", "W2"], [pk])
                    pss.append(ps)
                    pks.append(pk)
                xres, xresk = xr.next()
                S.dma(xres, scr["xmid"][s * 128:(s + 1) * 128, :], r=[f"xmid{s}"], w=[xresk])
                xo, xok = xor_.next()
                post_norm_res(pss, pks, g2[j], f"g2{j}", xres, xresk, xo, xok, rings)
                if s < 16:
                    S.dma(x_out[s * 128:(s + 1) * 128, :], xo, r=[xok], w=[f"xout{s}"])
                    if scatter_to is not None:
                        idx_ap = self.idx_s[:, s:s + 1]
                        S.op("pool", lambda e, xo=xo, idx_ap=idx_ap: e.indirect_dma_start(
                            out=scatter_to[:, :], out_offset=bass.IndirectOffsetOnAxis(ap=idx_ap, axis=0),
                            in_=xo, in_offset=None), [xok, "idx_s"], ["SHXw"], dma=True)
                else:
                    S.dma(xc_out[(s - 16) * 128:(s - 15) * 128, :], xo, r=[xok], w=[f"xcout{s}"])
        self.barrier()
        m.release(mkG)


def _bf(a):
    return np.ascontiguousarray(a.astype(ml_dtypes.bfloat16))


_CONST_CACHE = {}


def host_consts():
    if _CONST_CACHE:
        return _CONST_CACHE
    out = {}
    inv_freq = (10000.0 ** (-np.arange(16, dtype=np.float64) / 16))
    nat = np.arange(T_LAT)
    for half in range(2):
        pos = (nat + half * 2048) % T_LAT
        row = (pos // 64).astype(np.float64)
        col = (pos % 64).astype(np.float64)
        ang = np.stack([row[:, None] * inv_freq, col[:, None] * inv_freq], axis=1)
        cos = np.cos(ang).reshape(T_LAT, 32).astype(np.float32)
        sin = np.sin(ang).reshape(T_LAT, 32).astype(np.float32)
        out[f"cos{half}"] = np.ascontiguousarray(cos.reshape(32, 128, 32).transpose(1, 0, 2))
        out[f"sin{half}"] = np.ascontiguousarray(sin.reshape(32, 128, 32).transpose(1, 0, 2))
        lq = (np.arange(T_OWN) + half * 2048).astype(np.int64)
        prod = (pos.astype(np.int64)[:, None] * lq[None, :]) % T_LAT
        angd = 2.0 * np.pi * prod.astype(np.float64) / T_LAT
        out[f"dftc{half}"] = _bf(np.cos(angd) / 64.0)
        out[f"dfts{half}"] = _bf(np.sin(angd) / 64.0)
        out[f"dftcs{half}"] = np.ascontiguousarray(np.concatenate([out[f"dftc{half}"][2048:], out[f"dftc{half}"][:2048]], axis=0))
        out[f"dftss{half}"] = np.ascontiguousarray(np.concatenate([out[f"dfts{half}"][2048:], out[f"dfts{half}"][:2048]], axis=0))
        mk = np.zeros((128, 4), np.float32)
        mk[:, 0] = float(half)
        mk[:, 1] = float(1 - half)
        mk[:, 2] = float(half)
        mk[:, 3] = float(1 - half)
        out[f"masks{half}"] = mk
    pc = (np.arange(256)[:, None] * np.arange(256)[None, :]) % 256
    angc = 2.0 * np.pi * pc / 256.0
    out["dftcc"] = _bf(np.cos(angc) / 16.0)
    out["dftsc"] = _bf(np.sin(angc) / 16.0)
    p64 = (np.arange(64)[:, None] * np.arange(64)[None, :]) % 64
    a64 = 2.0 * np.pi * p64 / 64.0
    cbd = np.zeros((128, 128))
    sbd = np.zeros((128, 128))
    for g in range(2):
        cbd[g * 64:(g + 1) * 64, g * 64:(g + 1) * 64] = np.cos(a64) / 8.0
        sbd[g * 64:(g + 1) * 64, g * 64:(g + 1) * 64] = -np.sin(a64) / 8.0
    out["cbd"] = _bf(cbd)
    out["nsbd"] = _bf(sbd)
    out["ident"] = _bf(np.eye(128))
    _CONST_CACHE.update(out)
    return out


LAYER_W = ["wmod", "w_in", "w_brf", "w_brd", "w_brg", "w_brc", "w_out", "w_ff1", "w_ff2"]


def layer_host_inputs(l, inp):
    d = {}
    d["wmod"] = np.ascontiguousarray(inp["w_mod"][l])
    d["w_in"] = np.ascontiguousarray(inp["w_in"][l])
    d["w_brf"] = np.ascontiguousarray(inp["w_br_fourier"][l])
    d["w_brd"] = np.ascontiguousarray(inp["w_br_diff"][l])
    d["w_brg"] = np.ascontiguousarray(inp["w_br_gqa"][l])
    d["w_brc"] = np.ascontiguousarray(inp["w_br_conv"][l])
    d["w_out"] = np.ascontiguousarray(inp["w_out"][l])
    d["w_ff1"] = np.ascontiguousarray(inp["w_ff1"][l])
    d["w_ff2"] = np.ascontiguousarray(inp["w_ff2"][l])
    bm = inp["b_mod"][l]
    rowsA = np.concatenate([inp["g_post_mix"][l], inp["g_post_mlp"][l], bm[2048:3072], bm[5120:6144]])
    d["rowsA"] = np.ascontiguousarray(np.broadcast_to(rowsA[None, :], (128, 4096)))
    rowsB = np.concatenate([inp["q_norm"][l], inp["k_norm"][l], inp["diff_subln"][l], inp["conv_dw_bias"][l],
                            inp["conv_ln_g"][l], inp["conv_ln_b"][l], inp["diff_lambda"][l].reshape(-1)])
    d["rowsB"] = np.ascontiguousarray(np.broadcast_to(rowsB[None, :], (128, 1664)))
    d["bmodfm"] = np.ascontiguousarray(bm.reshape(48, 128).T)
    fm = np.zeros((128, 109), np.float32)
    fm[:, 0:8] = inp["g_pre_mix"][l].reshape(8, 128).T
    fm[:, 8:16] = inp["g_pre_mlp"][l].reshape(8, 128).T
    dw = inp["conv_dw"][l]
    fm[:, 16:109] = dw.reshape(31, 3, 128).transpose(2, 1, 0).reshape(128, 93)
    d["fmv"] = fm
    return d


def declare_layer_io(nc, sfx, with_dft_ctx=True):
    io = {}

    def din(name, shape, dt):
        io[name] = nc.dram_tensor(name + sfx, shape, dt, kind="ExternalInput").ap()
    din("wmod", [1024, 6144], F32)
    din("w_in", [1024, IN_COLS], F32)
    din("w_brf", [384, 1024], F32)
    din("w_brd", [512, 1024], F32)
    din("w_brg", [512, 1024], F32)
    din("w_brc", [384, 1024], F32)
    din("w_out", [1024, 1024], F32)
    din("w_ff1", [1024, 4096], F32)
    din("w_ff2", [4096, 1024], F32)
    din("rowsA", [128, 4096], F32)
    din("rowsB", [128, 1664], F32)
    din("bmodfm", [128, 48], F32)
    din("fmv", [128, 109], F32)
    return io


def declare_common_io(nc):
    io = {}

    def din(name, shape, dt):
        io[name] = nc.dram_tensor(name, shape, dt, kind="ExternalInput").ap()
    din("sc2", [128, 8, 2], F32)
    din("cos", [128, 32, 32], F32)
    din("sin", [128, 32, 32], F32)
    din("masks", [128, 4], F32)
    din("dftc", [4096, 2048], BF16)
    din("dfts", [4096, 2048], BF16)
    din("dftcc", [256, 256], BF16)
    din("dftsc", [256, 256], BF16)
    din("cbd", [128, 128], BF16)
    din("nsbd", [128, 128], BF16)
    din("ident", [128, 128], BF16)
    return io


def declare_scratch(nc, sfx=""):
    import os
    kind = "ExternalOutput" if os.environ.get("DBG", "") != "" else "Internal"

    def dsc(name, shape, dt):
        return nc.dram_tensor(name + sfx, shape, dt, kind=kind).ap()
    return {
        "KTd": dsc("KTd", [4, 128, NKEY], BF16),
        "Vd": dsc("Vd", [4, 128, 34, 129], BF16),
        "KTg": dsc("KTg", [2, 128, NKEY], BF16),
        "Vg": dsc("Vg", [2, 128, 34, 65], BF16),
        "U": dsc("U", [128, 34, 384], BF16),
        "gates": dsc("gates", [4, 128, 1024], F32),
        "xmid": dsc("xmid", [2304, 1024], F32),
        "W1b": dsc("W1b", [1024, 4096], BF16),
    }


def build_single_layer(l, stop_after=None):
    nc = bass.Bass("TRN2", target_bir_lowering=False)
    io = declare_common_io(nc)
    io.update(declare_layer_io(nc, ""))
    x_all = nc.dram_tensor("x_all", [NKEY, 1024], F32, kind="ExternalInput").ap()
    x_out = nc.dram_tensor("x_out", [T_OWN, 1024], F32, kind="ExternalOutput").ap()
    ctx_full = l < DEPTH - 1
    xc_out = nc.dram_tensor("xc_out", [T_CTX, 1024], F32, kind="ExternalOutput").ap() if ctx_full else None
    scr = declare_scratch(nc)
    B = Builder(nc)
    B.load_consts(io)
    B.layer(l, io, x_all, x_out, xc_out, scr, ctx_full, stop_after=stop_after)
    B.S.emit(nc)
    return nc


def common_core_inputs(inp, core):
    b, half = core // 2, core % 2
    hc = host_consts()
    sc = np.stack([inp["c"][b], inp["c_ctx"]], axis=1)
    sc2 = np.ascontiguousarray(sc.reshape(8, 128, 2).transpose(1, 0, 2))
    return {
        "sc2": sc2,
        "cos": hc[f"cos{half}"], "sin": hc[f"sin{half}"], "masks": hc[f"masks{half}"],
        "dftc": hc[f"dftc{half}"], "dfts": hc[f"dfts{half}"],
        "dftcc": hc["dftcc"], "dftsc": hc["dftsc"], "cbd": hc["cbd"], "nsbd": hc["nsbd"], "ident": hc["ident"],
    }


def build_fused():
    nc = bass.Bass("TRN2", target_bir_lowering=False)
    io = declare_common_io(nc)
    io["idxs"] = nc.dram_tensor("idxs", [128, 33], I32, kind="ExternalInput").ap()
    io["tok"] = nc.dram_tensor("tok", [128, 64], I32, kind="ExternalInput").ap()
    tabsA = {k: io[k] for k in ("cos", "sin", "dftc", "dfts", "masks")}
    io0 = dict(io)
    io0.update(declare_layer_io(nc, "_l0"))
    io1 = dict(io)
    io1.update(declare_layer_io(nc, "_l1"))
    x_all = nc.dram_tensor("x_all", [NKEY, 1024], F32, kind="ExternalInput").ap()
    out = nc.dram_tensor("x_out", [T_OWN, 1024], F32, kind="ExternalOutput").ap()
    X1 = nc.dram_tensor("X1", [NKEY, 1024], F32, kind="Internal").ap()
    SHX = nc.dram_tensor("SHX", [T_LAT, 1024], F32, kind="Internal", addr_space="Shared").ap()
    FLG = nc.dram_tensor("FLG", [256, 64], I32, kind="Internal", addr_space="Shared").ap()
    scr = declare_scratch(nc)
    B = Builder(nc)
    B.load_consts(io)
    S = B.S
    B.layer(0, io0, x_all, X1[0:2048, :], X1[4096:NKEY, :], scr, True, tabs=tabsA, scatter_to=SHX)
    S.op("pool", lambda e: e.indirect_dma_start(
        out=FLG[:, :], out_offset=bass.IndirectOffsetOnAxis(ap=B.idx_s[:, 32:33], axis=0),
        in_=B.tok_s, in_offset=None), ["idx_s", "tok_s"], ["FLGw"], dma=True)

    def poll(g):
        with g.register("tk") as tk, g.register("f0") as f0, g.register("dd") as dd:
            g.reg_load(tk, B.tok_s[0:1, 0:1])
            for row in (0, 128):
                g.reg_mov(dd, 1)
                with g.While(dd):
                    g.reg_load(f0, FLG[row:row + 1, 0:1])
                    g.reg_sub(dd, f0, tk)
        return g.memset(B.bar2, 0.0)
    S.op("pool", poll, ["FLGw", "tok_s"], ["SHXready"])
    B.layer(1, io1, X1, out, None, scr, False, tabs=tabsA, gather_from=SHX)
    S.emit(nc)
    return nc


_NC_CACHE = {}


def kernel(**inputs):
    inp = {k: np.asarray(v) for k, v in inputs.items()}
    x = inp["x"].astype(np.float32, copy=False)
    xc = inp["ctx"].astype(np.float32, copy=False)
    n = 8
    if "nc" not in _NC_CACHE:
        _NC_CACHE["nc"] = build_fused()
    nc = _NC_CACHE["nc"]
    lw0 = layer_host_inputs(0, inp)
    lw1 = layer_host_inputs(1, inp)
    tok = np.full((128, 64), int(np.random.default_rng().integers(1, 2 ** 30)), np.int32)
    hc = host_consts()
    in_maps = []
    for core in range(n):
        b, half = core // 2, core % 2
        own = x[b, half * 2048:(half + 1) * 2048]
        oth = x[b, (1 - half) * 2048:(2 - half) * 2048]
        mp = dict(common_core_inputs(inp, core))
        for k, v in lw0.items():
            mp[k + "_l0"] = v
        for k, v in lw1.items():
            mp[k + "_l1"] = v
        idxs = np.zeros((128, 33), np.int32)
        pp_ = np.arange(128, dtype=np.int32)
        for s_ in range(16):
            idxs[:, s_] = half * 2048 + s_ * 128 + pp_
            idxs[:, 16 + s_] = (1 - half) * 2048 + s_ * 128 + pp_
        idxs[:, 32] = half * 128 + pp_
        mp["idxs"] = idxs
        mp["tok"] = tok
        mp["x_all"] = np.ascontiguousarray(np.concatenate([own, oth, xc[b]], axis=0))
        in_maps.append(mp)
    res = run_bass_kernel_spmd(nc, in_maps, core_ids=list(range(n)))
    out = np.empty_like(x)
    for core in range(n):
        b, half = core // 2, core % 2
        out[b, half * 2048:(half + 1) * 2048] = res.results[core]["x_out"]
    return out
```

```python
import math
import contextlib
import numpy as np
import ml_dtypes
import concourse.bass as bass
import concourse.mybir as mybir
from concourse.bass_utils import run_bass_kernel_spmd

F32 = mybir.dt.float32
BF16 = mybir.dt.bfloat16
U8 = mybir.dt.uint8
AF = mybir.ActivationFunctionType
ALU = mybir.AluOpType
AX = mybir.AxisListType

D = 1024
DEPTH = 2
T_OWN = 2048
T_LAT = 4096
T_CTX = 256
NT_ALL = 34
NKEY = 4352
EPS = 1e-6
IN_COLS = 7552
GATE_COL0 = 3456
ARENA = 206 * 1024

ENGS = ("pe", "act", "dve", "pool", "sp")
NSEM_ENG = 4
NSEM_DMA = 12


class Op:
    __slots__ = ("eng", "fn", "dma", "deps", "sig", "sig_idx", "dma_idx")

    def __init__(self, eng, fn, dma):
        self.eng = eng
        self.fn = fn
        self.dma = dma
        self.deps = []
        self.sig = False
        self.sig_idx = -1
        self.dma_idx = -1


class Sched:
    def __init__(self):
        self.ops = {e: [] for e in ENGS}
        self.state = {}
        self.fence = None
        self.dmas_since = []
        self.last = {}

    def _st(self, k):
        s = self.state.get(k)
        if s is None:
            s = [[], []]
            self.state[k] = s
        return s

    def capture_start(self):
        self._cap = []

    def capture_end(self):
        c = self._cap
        self._cap = None
        return c

    def replay_interleaved(self, caps, depth=2):
        if not caps:
            return
        n = max(len(c) for c in caps)
        step = max(1, n // depth)
        recs = []
        for t, c in enumerate(caps):
            for i, r in enumerate(c):
                recs.append((t * step + i, t, i, r))
        recs.sort(key=lambda x: (x[0], x[1], x[2]))
        for _, _, _, r in recs:
            self.op(*r)

    def op(self, eng, fn, reads=(), writes=(), dma=False, extra=()):
        if getattr(self, "_cap", None) is not None:
            self._cap.append((eng, fn, tuple(reads), tuple(writes), dma, tuple(extra)))
            return None
        o = Op(eng, fn, dma)
        deps = {}
        for k in reads:
            s = self.state.get(k)
            if s:
                for w in s[0]:
                    deps[id(w)] = w
        accum = set()
        for k in writes:
            s = self.state.get(k)
            if s:
                if dma and not s[1] and s[0] and all(w.dma for w in s[0]):
                    accum.add(k)
                    continue
                for w in s[0]:
                    deps[id(w)] = w
                for r in s[1]:
                    deps[id(r)] = r
        for d in extra:
            deps[id(d)] = d
        if self.fence is not None:
            deps[id(self.fence)] = self.fence
        for d in deps.values():
            if (not d.dma) and (not dma) and d.eng == eng and eng == "pe":
                continue
            o.deps.append(d)
            d.sig = True
        for k in reads:
            rl = self._st(k)[1]
            if not dma:
                rl[:] = [x for x in rl if x.dma or x.eng != eng]
            rl.append(o)
        for k in writes:
            s = self._st(k)
            if k in accum:
                s[0] = s[0] + [o]
            else:
                s[0] = [o]
            s[1] = []
        self.ops[eng].append(o)
        if dma:
            self.dmas_since.append(o)
        else:
            self.last[eng] = o
        return o

    def pe(self, fn, r=(), w=()):
        return self.op("pe", fn, r, w)

    def act(self, fn, r=(), w=()):
        return self.op("act", fn, r, w)

    def dve(self, fn, r=(), w=()):
        return self.op("dve", fn, r, w)

    def pool(self, fn, r=(), w=()):
        return self.op("pool", fn, r, w)

    def dma(self, out, in_, r=(), w=(), q="sp"):
        return self.op(q, lambda e: e.dma_start(out=out, in_=in_), r, w, dma=True)

    def barrier(self, scratch):
        import os
        v = os.environ.get("BARV", "")
        extra = list(self.last.values()) + list(self.dmas_since)
        if v == "nodma":
            extra = list(self.last.values())
        if v == "onlydma":
            extra = list(self.dmas_since)
        self.fence = None
        o = self.op("dve", lambda e: e.memset(scratch, 0.0), (), (), extra=extra)
        o.sig = True
        self.fence = o
        self.dmas_since = []
        self.state = {}

    def emit(self, nc):
        for e in ENGS:
            si = 0
            di = 0
            for o in self.ops[e]:
                if o.dma:
                    o.dma_idx = di
                    di += 1
                elif o.sig:
                    o.sig_idx = si
                    si += 1
        with contextlib.ExitStack() as es:
            esem = {e: [es.enter_context(nc.semaphore(f"s_{e}_{i}")) for i in range(NSEM_ENG)]
                    for e in ("pe", "act", "dve", "pool")}
            dsem = {e: [es.enter_context(nc.semaphore(f"d_{e}_{i}")) for i in range(NSEM_DMA)]
                    for e in ("sp", "pool", "act")}
            block = es.enter_context(nc.Block())

            def run(e, eng):
                known = {x: -1 for x in ("pe", "act", "dve", "pool")}
                dknown = {}
                for o in self.ops[e]:
                    for d in o.deps:
                        if d.dma:
                            slot = d.dma_idx % NSEM_DMA
                            val = 16 * (d.dma_idx // NSEM_DMA + 1)
                            if dknown.get((d.eng, slot), 0) >= val:
                                continue
                            eng.wait_ge(dsem[d.eng][slot], val)
                            dknown[(d.eng, slot)] = val
                        else:
                            if known[d.eng] >= d.sig_idx:
                                continue
                            eng.wait_ge(esem[d.eng][d.sig_idx % NSEM_ENG], d.sig_idx // NSEM_ENG + 1)
                            known[d.eng] = d.sig_idx
                    if o.dma:
                        slot = o.dma_idx % NSEM_DMA
                        prev = 16 * (o.dma_idx // NSEM_DMA)
                        if prev > 0 and dknown.get((e, slot), 0) < prev:
                            eng.wait_ge(dsem[e][slot], prev)
                            dknown[(e, slot)] = prev
                        ins = o.fn(eng)
                        ins.then_inc(dsem[e][slot], 16)
                    else:
                        ins = o.fn(eng)
                        if o.sig:
                            ins.then_inc(esem[e][o.sig_idx % NSEM_ENG], 1)
                nd = sum(1 for o in self.ops[e] if o.dma)
                for slot in range(min(nd, NSEM_DMA)):
                    cnt = (nd - slot + NSEM_DMA - 1) // NSEM_DMA
                    if dknown.get((e, slot), 0) < 16 * cnt:
                        eng.wait_ge(dsem[e][slot], 16 * cnt)

            @block.tensor
            def _(eng):
                run("pe", eng)

            @block.scalar
            def _(eng):
                run("act", eng)

            @block.vector
            def _(eng):
                run("dve", eng)

            @block.gpsimd
            def _(eng):
                run("pool", eng)

            @block.sync
            def _(eng):
                run("sp", eng)


I32 = mybir.dt.int32


def _dsize(dt):
    return 4 if dt in (F32, I32) else 2


class Mem:
    def __init__(self, arena, size):
        self.a = arena
        self.off = 0
        self.size = size

    def alloc(self, shape, dt):
        n = int(np.prod(shape))
        nb = n * _dsize(dt)
        start = self.off
        self.off += (nb + 63) // 64 * 64
        assert self.off <= self.size, f"SBUF arena overflow {self.off} > {self.size}"
        v = self.a[:, start:start + nb].bitcast(dt)
        return view(v, shape)

    def mark(self):
        return self.off

    def release(self, m):
        self.off = m


def view(v, shape):
    if len(shape) == 1:
        return v
    names = "abcde"[:len(shape)]
    kw = {names[i]: int(shape[i]) for i in range(len(shape) - 1)}
    return v.rearrange(f"p ({' '.join(names)}) -> p {' '.join(names)}", **kw)


class Ring:
    def __init__(self, mem, n, shape, dt, name):
        self.bufs = [mem.alloc(shape, dt) for _ in range(n)]
        self.n = n
        self.i = 0
        self.name = name

    def next(self):
        j = self.i % self.n
        self.i += 1
        return self.bufs[j], f"{self.name}{j}"


class PRing:
    def __init__(self, banks, idxs):
        self.banks = banks
        self.idxs = idxs
        self.i = 0

    def next(self):
        j = self.idxs[self.i % len(self.idxs)]
        self.i += 1
        return self.banks[j], f"pb{j}"


class Builder:
    def __init__(self, nc):
        self.nc = nc
        self.S = Sched()
        S = self.S
        self.arena = nc.alloc_sbuf_tensor("arena", [128, ARENA], U8)
        self.mem = Mem(self.arena, ARENA)
        self.pp = [nc.alloc_psum_tensor(f"pp{i}", [128, 1024], F32).ap() for i in range(4)]
        self.pb = [self.pp[i // 2][:, (i % 2) * 512:(i % 2 + 1) * 512] for i in range(8)]
        m = self.mem
        self.ident = m.alloc([128], BF16)
        self.cbd = m.alloc([128], BF16)
        self.nsbd = m.alloc([128], BF16)
        self.masks = m.alloc([4], F32)
        self.rowsB = m.alloc([1664], F32)
        self.fmv = m.alloc([109], F32)
        self.AB = m.alloc([2, 4, 8], F32)
        self.lamt = m.alloc([8], F32)
        self.sublnS = m.alloc([128], F32)
        self.bar = m.alloc([16], F32)
        self.bar2 = m.alloc([16], F32)
        self.idx_s = m.alloc([33], I32)
        self.tok_s = m.alloc([64], I32)
        self.hT = m.alloc([8, 2304], BF16)
        self.base = m.mark()

    def MM(self, out, lhsT, rhs, start, stop, r, w):
        self.S.pe(lambda e: e.matmul(out, lhsT=lhsT, rhs=rhs, start=start, stop=stop), r, w)

    def TR(self, out, in_, r, w):
        ident = self.ident
        self.S.pe(lambda e: e.transpose(out, in_, ident), list(r) + ["ident"], w)

    def ACT(self, out, in_, func, r, w, scale=None, bias=None, accum=None):
        kw = {}
        if scale is not None:
            kw["scale"] = scale
        if bias is not None:
            kw["bias"] = bias
        if accum is not None:
            kw["accum_out"] = accum
        self.S.act(lambda e: e.activation(out=out, in_=in_, func=func, **kw), r, w)

    def TT(self, eng, out, in0, in1, op, r, w):
        self.S.op(eng, lambda e: e.tensor_tensor(out=out, in0=in0, in1=in1, op=op), r, w)

    def TS(self, eng, out, in0, s1, s2, op0, op1, r, w):
        if s2 is None:
            self.S.op(eng, lambda e: e.tensor_scalar(out=out, in0=in0, scalar1=s1, scalar2=None, op0=op0), r, w)
        else:
            self.S.op(eng, lambda e: e.tensor_scalar(out=out, in0=in0, scalar1=s1, scalar2=s2, op0=op0, op1=op1), r, w)

    def STT(self, eng, out, in0, scalar, in1, op0, op1, r, w):
        self.S.op(eng, lambda e: e.scalar_tensor_tensor(out=out, in0=in0, scalar=scalar, in1=in1, op0=op0, op1=op1), r, w)

    def CP(self, eng, out, in_, r, w):
        if eng == "act":
            self.S.act(lambda e: e.activation(out=out, in_=in_, func=AF.Copy), r, w)
        else:
            self.S.op(eng, lambda e: e.tensor_copy(out=out, in_=in_), r, w)

    def RECIP(self, out, in_, r, w):
        self.S.dve(lambda e: e.reciprocal(out=out, in_=in_), r, w)

    def rstd(self, ss, n, out, tmp, r, w):
        self.TS("dve", tmp, ss, 1.0 / n, EPS, ALU.mult, ALU.add, r, w)
        self.ACT(tmp, tmp, AF.Sqrt, w, w)
        self.RECIP(out, tmp, w, w)

    def barrier(self):
        self.S.barrier(self.bar)

    def dump(self, name, ap):
        import os
        if os.environ.get("DBG", "") == "":
            return
        t = self.nc.dram_tensor("dbg_" + name, [128] + [int(x) for x in ap.shape[1:]], ap.dtype, kind="ExternalOutput").ap()
        self.S.dma(t, ap)
        self.barrier()

    def load_consts(self, io):
        S = self.S
        S.dma(self.ident, io["ident"], w=["ident"])
        S.dma(self.cbd, io["cbd"], w=["cbd"])
        S.dma(self.nsbd, io["nsbd"], w=["nsbd"])
        if "idxs" in io:
            S.dma(self.idx_s, io["idxs"], w=["idx_s"])
            S.dma(self.tok_s, io["tok"], w=["tok_s"])

    def layer(self, l, io, x_all, x_out, xc_out, scr, ctx_full, stop_after=None, tabs=None, reuse_kv=False, skip_setup=False, scatter_to=None, gather_from=None):
        S = self.S
        m = self.mem
        pb = self.pb
        lam_init = 0.8 - 0.6 * math.exp(-0.3 * l)
        n_slots = 18 if ctx_full else 16
        hT = self.hT
        if tabs is None:
            tabs = io

        def pbf(i):
            return pb[i].bitcast(BF16)

        self.barrier()
        mk_layer = m.mark()
        hTh = m.alloc([8, 256], BF16)
        vT = m.alloc([3, 2078], BF16)
        vTc = m.alloc([3, 286], BF16)
        mk = m.mark()
        S.dma(self.rowsB, io["rowsB"], w=["rowsB"])
        S.dma(self.masks, tabs["masks"], w=["masks"])
        S.dma(self.fmv, io["fmv"], w=["fmv"])
        AB = self.AB
        lt = self.lamt
        if not skip_setup:
            rowsA = m.alloc([4096], F32)
            S.dma(rowsA, io["rowsA"], w=["rowsA"])
            bmodfm = m.alloc([48], F32)
            S.dma(bmodfm, io["bmodfm"], w=["bmodfm"])
            sc = m.alloc([8, 2], F32)
            S.dma(sc, io["sc2"], w=["sc"])
            self.ACT(sc, sc, AF.Silu, ["sc"], ["sc"])
            screp = [m.alloc([8, 128], F32) for _ in range(2)]
            ones_t = m.alloc([128], F32)
            S.dve(lambda e: e.memset(ones_t, 1.0), [], ["ones_t"])
            for j in range(2):
                for c in range(8):
                    self.TS("dve", screp[j][:, c, :], ones_t, sc[:, c, j:j + 1], None, ALU.mult, None, ["sc", "ones_t"], [f"screp{j}"])
            if stop_after == "s1":
                return
            wmr = Ring(m, 2, [8, 512], F32, "wm")
            growr = Ring(m, 2, [512], F32, "grow")
            gtmp = m.alloc([512], F32)
            pg = PRing(pb, [1, 2])
            wmod_v = io["wmod"].rearrange("(c p) n -> p c n", p=128)
            modfm = m.alloc([48, 2], F32)
            for cg in range(12):
                wm, wk = wmr.next()
                S.dma(wm, wmod_v[:, :, cg * 512:(cg + 1) * 512], w=[wk])
                mi = cg // 2
                if mi in (2, 5):
                    for j in range(2):
                        ps, pk = pg.next()
                        for c in range(8):
                            self.MM(ps, screp[j][:, c, :], wm[:, c, :], c == 0, c == 7, [f"screp{j}", wk], [pk])
                        ro = (2048 if mi == 2 else 3072) + (cg % 2) * 512
                        go = (0 if mi == 2 else 1024) + (cg % 2) * 512
                        self.TT("dve", gtmp, ps, rowsA[:, ro:ro + 512], ALU.add, [pk, "rowsA"], ["gtmp"])
                        grow, gk = growr.next()
                        self.TT("pool", grow, gtmp, rowsA[:, go:go + 512], ALU.mult, ["gtmp", "rowsA"], [gk])
                        gi = (0 if mi == 2 else 2) + j
                        S.dma(scr["gates"][gi, :, (cg % 2) * 512:(cg % 2) * 512 + 512], grow, r=[gk], w=[f"gates{gi}"])
                else:
                    for fcc in range(4):
                        idx = cg * 4 + fcc
                        for c in range(8):
                            self.MM(pb[0][:, idx * 2:idx * 2 + 2], wm[:, c, fcc * 128:(fcc + 1) * 128], sc[:, c, :],
                                    c == 0, c == 7, [wk, "sc"], ["pb0"])
            if stop_after == "s2":
                return
            pmod = pb[0][:, 0:96].rearrange("p (a b) -> p a b", b=2)
            for cs in ((0, 16), (24, 40)):
                a, b = cs
                for j in range(2):
                    self.TT("dve", modfm[:, a:b, j], pmod[:, a:b, j], bmodfm[:, a:b], ALU.add, ["pb0", "bmodfm"], ["modfm"])
            if stop_after == "s3":
                return
            AB = self.AB
            gpre_mix = self.fmv[:, 0:8]
            gpre_mlp = self.fmv[:, 8:16]
            for j in range(2):
                self.STT("dve", AB[:, j, 0, :], modfm[:, 8:16, j], 1.0, gpre_mix, ALU.add, ALU.mult, ["modfm", "fmv"], ["AB"])
                self.CP("dve", AB[:, j, 1, :], modfm[:, 0:8, j], ["modfm"], ["AB"])
                self.STT("dve", AB[:, j, 2, :], modfm[:, 32:40, j], 1.0, gpre_mlp, ALU.add, ALU.mult, ["modfm", "fmv"], ["AB"])
                self.CP("dve", AB[:, j, 3, :], modfm[:, 24:32, j], ["modfm"], ["AB"])
            if stop_after == "s4":
                return
            lamrow = self.rowsB[:, 1408:1664].rearrange("p (a b) -> p a b", a=4)
            lt = self.lamt
            lp = m.alloc([2, 64], F32)
            self.TT("dve", lp[:, 0, :], lamrow[:, 0, :], lamrow[:, 1, :], ALU.mult, ["rowsB"], ["lp"])
            self.TT("dve", lp[:, 1, :], lamrow[:, 2, :], lamrow[:, 3, :], ALU.mult, ["rowsB"], ["lp"])
            S.dve(lambda e: e.tensor_reduce(out=lt[:, 0:2], in_=lp, axis=AX.X, op=ALU.add), ["lp"], ["lamt"])
            if stop_after == "s45":
                return
            self.ACT(lt[:, 2:4], lt[:, 0:2], AF.Exp, ["lamt"], ["lamt"])
            self.TT("dve", lt[:, 4:5], lt[:, 2:3], lt[:, 3:4], ALU.subtract, ["lamt"], ["lamt"])
            self.TS("dve", lt[:, 5:6], lt[:, 4:5], -1.0, -lam_init, ALU.mult, ALU.add, ["lamt"], ["lamt"])
            self.TS("dve", self.sublnS, self.rowsB[:, 128:256], 1.0 - lam_init, None, ALU.mult, None, ["rowsB"], ["sublnS"])
        lamneg = lt[:, 5:6]
        if stop_after == "s5":
            return
        qn_row = self.rowsB[:, 0:64]
        kn_row = self.rowsB[:, 64:128]
        cbias_row = self.rowsB[:, 256:640]
        clng_row = self.rowsB[:, 640:1024]
        clnb_row = self.rowsB[:, 1024:1408]
        self.barrier()
        m.release(mk)

        def norm_mod_tile(xt, xk, abi, j, hdst, hk, rings):
            st, sk = rings["st"].next()
            junk, jk = rings["junk"].next()
            self.ACT(junk, xt, AF.Square, [xk], [jk, sk], accum=st[:, 0:1])
            self.rstd(st[:, 0:1], 1024.0, st[:, 2:3], st[:, 1:2], [sk], [sk])
            xn, nk = rings["xn"].next()
            self.TS("dve", xn, xt, st[:, 2:3], None, ALU.mult, None, [xk, sk], [nk])
            pT = pbf(0).rearrange("p (a b) -> p a b", a=8)
            for c in range(8):
                self.TR(pT[:, c, :], xn[:, c * 128:(c + 1) * 128], [nk], ["pb0"])
            for c in range(8):
                self.ACT(hdst[:, c, :], pT[:, c, :], AF.Identity, ["pb0", "AB"], [hk],
                         scale=AB[:, j, 2 * abi, c:c + 1], bias=AB[:, j, 2 * abi + 1, c:c + 1])

        def qk_post(ps, pk, H, normrow, rope_t, out_bf, ok, rings):
            xf, xk = rings["xf"].next()
            xf = xf[:, 0:H * 64]
            self.CP("act", xf, ps, [pk], [xk])
            x3 = xf.rearrange("p (h d) -> p h d", h=H)
            if normrow is not None:
                sq, qk = rings["xf"].next()
                st, sk = rings["st"].next()
                for h in range(H):
                    self.ACT(sq[:, 0:64], x3[:, h, :], AF.Square, [xk], [qk, sk], accum=st[:, h:h + 1])
                self.rstd(st[:, 0:H], 64.0, st[:, 16:16 + H], st[:, 8:8 + H], [sk], [sk])
                for h in range(H):
                    self.STT("dve", x3[:, h, :], x3[:, h, :], st[:, 16 + h:17 + h], normrow, ALU.mult, ALU.mult,
                             [xk, sk, "rowsB"], [xk])
            if rope_t is not None:
                x5 = xf.rearrange("p (h a r f) -> p h a r f", h=H, a=2, r=2)
                o5 = out_bf.rearrange("p h (a r f) -> p h a r f", a=2, r=2)
                re, im = x5[:, :, :, 0, :], x5[:, :, :, 1, :]
                cs = self.cosT[:, rope_t, :].rearrange("p (a f) -> p a f", a=2).unsqueeze(1).broadcast_to([128, H, 2, 16])
                sn = self.sinT[:, rope_t, :].rearrange("p (a f) -> p a f", a=2).unsqueeze(1).broadcast_to([128, H, 2, 16])
                t1, k1 = rings["rt"].next()
                t2, k2 = rings["rt"].next()
                t1 = t1[:, 0:H * 32].rearrange("p (h a f) -> p h a f", h=H, a=2)
                t2 = t2[:, 0:H * 32].rearrange("p (h a f) -> p h a f", h=H, a=2)
                self.TT("dve", t1, re, cs, ALU.mult, [xk, "cosT"], [k1])
                self.TT("pool", t2, im, sn, ALU.mult, [xk, "sinT"], [k2])
                self.TT("dve", o5[:, :, :, 0, :], t1, t2, ALU.subtract, [k1, k2], [ok])
                t3, k3 = rings["rt"].next()
                t4, k4 = rings["rt"].next()
                t3 = t3[:, 0:H * 32].rearrange("p (h a f) -> p h a f", h=H, a=2)
                t4 = t4[:, 0:H * 32].rearrange("p (h a f) -> p h a f", h=H, a=2)
                self.TT("dve", t3, im, cs, ALU.mult, [xk, "cosT"], [k3])
                self.TT("pool", t4, re, sn, ALU.mult, [xk, "sinT"], [k4])
                self.TT("pool", o5[:, :, :, 1, :], t3, t4, ALU.add, [k3, k4], [ok])
            else:
                self.CP("dve", out_bf, x3, [xk], [ok])

        if stop_after == "setup":
            return
        mk = m.mark()
        WB = m.alloc([8, 1664], BF16)
        win_v = io["w_in"].rearrange("(c p) n -> p c n", p=128)
        if reuse_kv:
            WB_loaded = False
        else:
            WB_loaded = True
        self.cosT = m.alloc([32, 32], F32)
        self.sinT = m.alloc([32, 32], F32)
        S.dma(self.cosT, tabs["cos"], w=["cosT"])
        S.dma(self.sinT, tabs["sin"], w=["sinT"])
        for c in range(8 if WB_loaded else 0):
            S.dma(WB[:, c, :], win_v[:, c, 0:1664], w=["WB"], q="pool")
        w1v = io["w_ff1"]
        for c in range(8):
            S.dma(scr["W1b"][c * 128:(c + 1) * 128, :], w1v[c * 128:(c + 1) * 128, :], w=["W1b"], q="pool")
        rings = {
            "st": Ring(m, 4, [32], F32, "st"),
            "junk": Ring(m, 1, [1024], BF16, "junk"),
            "xn": Ring(m, 2, [1024], BF16, "xn"),
            "xf": Ring(m, 8, [512], F32, "xf"),
            "rt": Ring(m, 16, [256], F32, "rt"),
        }
        hTtmp = Ring(m, 3, [8, 128], BF16, "hTt")
        vdr = Ring(m, 2, [4, 129], BF16, "vdst")
        vgr = Ring(m, 2, [2, 65], BF16, "vgst")
        ustr = Ring(m, 2, [384], BF16, "ust")
        kbr = Ring(m, 2, [512], BF16, "kb")
        kdr = Ring(m, 2, [2, 2, 64], BF16, "kdup")
        ktdr = Ring(m, 2, [4, 128], BF16, "ktdst")
        ktgr = Ring(m, 2, [2, 128], BF16, "ktgst")
        for b_ in vdr.bufs:
            S.dve(lambda e, b_=b_: e.memset(b_[:, :, 128:129], 1.0), [], ["vdst0", "vdst1"])
        for b_ in vgr.bufs:
            S.dve(lambda e, b_=b_: e.memset(b_[:, :, 64:65], 1.0), [], ["vgst0", "vgst1"])
        pr = PRing(pb, [2, 3, 4, 5, 6, 7])
        xr = Ring(m, 3, [1024], F32, "xt3")
        tiles_b1 = [t for t in range(NT_ALL) if not (reuse_kv and (t >= 32 or (t >= 16 and t not in (16, 31))))]
        xloads = {}

        def issue_xload(i):
            if i < len(tiles_b1) and i not in xloads:
                tt_ = tiles_b1[i]
                xt_, xk_ = xr.next()
                if gather_from is not None and 16 <= tt_ < 32:
                    idx_ap = self.idx_s[:, tt_:tt_ + 1]
                    S.op("pool", lambda e, xt_=xt_, idx_ap=idx_ap: e.indirect_dma_start(
                        out=xt_, out_offset=None, in_=gather_from[:, :],
                        in_offset=bass.IndirectOffsetOnAxis(ap=idx_ap, axis=0)), ["idx_s", "SHXready"], [xk_], dma=True)
                else:
                    S.dma(xt_, x_all[tt_ * 128:(tt_ + 1) * 128, :], w=[xk_])
                xloads[i] = (xt_, xk_)
        issue_xload(0)
        issue_xload(1)
        caps_b1 = []
        for ti_, t in enumerate(tiles_b1):
            kind = "own" if t < 16 else ("other" if t < 32 else "ctx")
            S.capture_start()
            caps_b1.append(S._cap)
            issue_xload(ti_ + 2)
            xt, xk = xloads[ti_]
            if kind == "own":
                hdst, hk = hT[:, :, t * 128:(t + 1) * 128], f"hT{t}"
            elif kind == "ctx":
                sl = 16 + (t - 32)
                hdst, hk = hT[:, :, sl * 128:(sl + 1) * 128], f"hT{sl}"
            elif t in (16, 31):
                hi = 0 if t == 16 else 1
                hdst, hk = hTh[:, :, hi * 128:(hi + 1) * 128], f"hTh{hi}"
            else:
                hdst, hk = hTtmp.next()
            j = 1 if kind == "ctx" else 0
            norm_mod_tile(xt, xk, 0, j, hdst, hk, rings)
            if reuse_kv:
                continue
            rope_t = None if kind == "ctx" else t

            def proj(c0, c1):
                ps, pk = pr.next()
                for c in range(8):
                    self.MM(ps[:, 0:c1 - c0], hdst[:, c, :], WB[:, c, c0:c1], c == 0, c == 7, [hk, "WB"], [pk])
                return ps, pk
            ps, pk = proj(0, 512)
            kb, kk = kbr.next()
            qk_post(ps, pk, 8, None, rope_t, kb.rearrange("p (h d) -> p h d", h=8), kk, rings)
            pT = pbf(1)[:, 0:512].rearrange("p (a b) -> p a b", a=4)
            for h in range(4):
                self.TR(pT[:, h, :], kb[:, h * 128:(h + 1) * 128], [kk], ["pb1"])
            ks, ksk = ktdr.next()
            self.CP("act", ks, pT, ["pb1"], [ksk])
            S.dma(scr["KTd"][:, :, t * 128:(t + 1) * 128].rearrange("h p k -> p h k"), ks, r=[ksk], w=["KTd"])
            ps, pk = proj(512, 1024)
            vs, vk = vdr.next()
            self.CP("act", vs[:, :, 0:128], ps.rearrange("p (h d) -> p h d", h=4), [pk], [vk])
            S.dma(scr["Vd"][:, :, t, :].rearrange("h p c -> p h c"), vs, r=[vk], w=["Vd"])
            ps, pk = proj(1024, 1280)
            vs, vk = vgr.next()
            self.CP("act", vs[:, :, 0:64], ps[:, 128:256].rearrange("p (h d) -> p h d", h=2), [pk], [vk])
            S.dma(scr["Vg"][:, :, t, :].rearrange("h p c -> p h c"), vs, r=[vk], w=["Vg"])
            kd, kdk = kdr.next()
            qk_post(ps[:, 0:128], pk, 2, kn_row, rope_t, kd[:, :, 0, :], kdk, rings)
            self.CP("pool", kd[:, :, 1, :], kd[:, :, 0, :], [kdk], [kdk])
            pT2 = pbf(1)[:, 512:768].rearrange("p (a b) -> p a b", a=2)
            for h in range(2):
                self.TR(pT2[:, h, :], kd[:, h, :, :].rearrange("p a b -> p (a b)"), [kdk], ["pb1"])
            ks, ksk = ktgr.next()
            self.CP("act", ks, pT2, ["pb1"], [ksk])
            S.dma(scr["KTg"][:, :, t * 128:(t + 1) * 128].rearrange("h p k -> p h k"), ks, r=[ksk], w=["KTg"])
            if kind != "ctx" or ctx_full:
                ps, pk = proj(1280, 1664)
                us, uk = ustr.next()
                self.CP("dve", us, ps[:, 0:384], [pk], [uk])
                S.dma(scr["U"][:, t, :], us, r=[uk], w=["U"])
        S.capture_end()
        S.replay_interleaved(caps_b1)
        self.barrier()
        m.release(mk)

        if stop_after == "B1":
            return
        mk = m.mark()
        mkB2 = mk
        QTd = m.alloc([4, 2304], BF16)
        QTg = m.alloc([4, 2304], BF16)
        OdT = m.alloc([4, 2304], BF16)
        OgT = m.alloc([4, 2304], BF16)
        mk_after_res = m.mark()
        W2B = m.alloc([8, 1792], BF16)
        self.cosT = m.alloc([32, 32], F32)
        self.sinT = m.alloc([32, 32], F32)
        S.dma(self.cosT, tabs["cos"], w=["cosT"])
        S.dma(self.sinT, tabs["sin"], w=["sinT"])
        for c in range(8):
            S.dma(W2B[:, c, :], win_v[:, c, 1664:3456], w=["W2B"], q="pool")
        rings = {
            "st": Ring(m, 4, [32], F32, "st"),
            "xf": Ring(m, 6, [512], F32, "xf"),
            "rt": Ring(m, 12, [256], F32, "rt"),
        }
        qbr = Ring(m, 2, [512], BF16, "qb")
        sgr = Ring(m, 2, [3, 128], F32, "sg")
        S.dve(lambda e: e.memset(vTc, 0.0), [], ["vTc"])
        pr = PRing(pb, [2, 3, 4, 5, 6, 7])

        def glu(hsrc, hk):
            pa, pak = pr.next()
            pg_, pgk = pr.next()
            pa3 = pa[:, 0:384].rearrange("p (a b) -> p a b", a=3)
            pg3 = pg_[:, 0:384].rearrange("p (a b) -> p a b", a=3)
            for cc in range(3):
                for c in range(8):
                    self.MM(pa3[:, cc, :], W2B[:, c, 1024 + cc * 128:1024 + (cc + 1) * 128], hsrc[:, c, :],
                            c == 0, c == 7, ["W2B", hk], [pak])
            for cc in range(3):
                for c in range(8):
                    self.MM(pg3[:, cc, :], W2B[:, c, 1408 + cc * 128:1408 + (cc + 1) * 128], hsrc[:, c, :],
                            c == 0, c == 7, ["W2B", hk], [pgk])
            sg, sgk = sgr.next()
            self.ACT(sg, pg3, AF.Sigmoid, [pgk], [sgk])
            return pa3, pak, sg, sgk

        caps_b2 = []
        for s in range(n_slots):
            S.capture_start()
            caps_b2.append(S._cap)
            hsrc, hk = hT[:, :, s * 128:(s + 1) * 128], f"hT{s}"
            rope_t = s if s < 16 else None
            for (c0, norm, QT, qkey) in ((0, None, QTd, "QTd"), (512, qn_row, QTg, "QTg")):
                ps, pk = pr.next()
                for c in range(8):
                    self.MM(ps, hsrc[:, c, :], W2B[:, c, c0:c0 + 512], c == 0, c == 7, [hk, "W2B"], [pk])
                qb, qk_ = qbr.next()
                qk_post(ps, pk, 8, norm, rope_t, qb.rearrange("p (h d) -> p h d", h=8), qk_, rings)
                pT = pbf(1)[:, 0:512].rearrange("p (a b) -> p a b", a=4)
                for h in range(4):
                    self.TR(pT[:, h, :], qb[:, h * 128:(h + 1) * 128], [qk_], ["pb1"])
                self.CP("act", QT[:, :, s * 128:(s + 1) * 128], pT, ["pb1"], [f"{qkey}{s}"])
            pa3, pak, sg, sgk = glu(hsrc, hk)
            if s < 16:
                self.TT("dve", vT[:, :, 15 + s * 128:15 + (s + 1) * 128], pa3, sg, ALU.mult, [pak, sgk], ["vT"])
            else:
                o = 15 + (s - 16) * 128
                self.TT("dve", vTc[:, :, o:o + 128], pa3, sg, ALU.mult, [pak, sgk], ["vTc"])
        for hi in range(2):
            S.capture_start()
            caps_b2.append(S._cap)
            hsrc, hk = hTh[:, :, hi * 128:(hi + 1) * 128], f"hTh{hi}"
            pa3, pak, sg, sgk = glu(hsrc, hk)
            tmp, tk = sgr.next()
            self.TT("dve", tmp, pa3, sg, ALU.mult, [pak, sgk], [tk])
            if hi == 0:
                self.TS("dve", vT[:, :, 2063:2078], tmp[:, :, 0:15], self.masks[:, 1:2], None, ALU.mult, None, [tk, "masks"], ["vT"])
            else:
                self.TS("dve", vT[:, :, 0:15], tmp[:, :, 113:128], self.masks[:, 0:1], None, ALU.mult, None, [tk, "masks"], ["vT"])
        S.capture_end()
        S.replay_interleaved(caps_b2)
        self.barrier()
        self.dump("hT", hT); self.dump("QTd", QTd); self.dump("QTg", QTg); self.dump("vT", vT); self.dump("vTc", vTc)
        self.dump("AB", self.AB); self.dump("lamt", self.lamt)
        m.release(mk_after_res)

        if stop_after == "B2":
            return
        mkC = m.mark()
        ktr = Ring(m, 2, [NKEY], BF16, "KT")
        vr = Ring(m, 2, [34 * 129], BF16, "V")
        ptr_ = Ring(m, 3, [512], BF16, "PT")
        afr = Ring(m, 2, [2, 2, 129], F32, "af")
        obr = Ring(m, 2, [2, 128], BF16, "ob")
        o32r = Ring(m, 2, [128], F32, "o32")
        ttr = Ring(m, 2, [128], F32, "tt")
        str_ = Ring(m, 4, [16], F32, "sta")
        sjunk = m.alloc([128], BF16)
        ps_s = self.pp[0].rearrange("p (j b q) -> p j b q", j=2, b=2)
        s_cnt = [0]
        accb = [[pb[4], pb[5]], [pb[6], pb[7]]]
        acck = [["pb4", "pb5"], ["pb6", "pb7"]]
        obhr = Ring(m, 2, [18, 128], BF16, "obh")
        qblocks = [(i * 256, list(range(34))) for i in range(8)]
        if ctx_full:
            qblocks.append((2048, [32, 33]))
        import os
        _cu = [int(x) for x in os.environ.get("CUNITS", "0,1,2,3,4,5,6,7").split(",")]
        _cqb = int(os.environ.get("CQB", "99"))
        _cpost = os.environ.get("CNOPOST", "") == ""
        qblocks = qblocks[:_cqb]
        for ui in _cu:
            diff = ui < 4
            idx = ui % 4
            dv = 128 if diff else 64
            kt, ktk = ktr.next()
            vb, vk = vr.next()
            if diff:
                S.dma(kt, scr["KTd"][idx], r=["KTd"], w=[ktk])
                v3 = vb.rearrange("p (t c) -> p t c", t=34)
                S.dma(v3, scr["Vd"][idx], r=["Vd"], w=[vk])
                QT, qkey, OT, okey = QTd, "QTd", OdT, "OdT"
            else:
                kvh = idx // 2
                S.dma(kt, scr["KTg"][kvh], r=["KTg"], w=[ktk])
                v3 = vb[:, 0:34 * 65].rearrange("p (t c) -> p t c", t=34)
                S.dma(v3, scr["Vg"][kvh], r=["Vg"], w=[vk])
                QT, qkey, OT, okey = QTg, "QTg", OgT, "OgT"
            units = [(q0, kcs, ki) for (q0, kcs) in qblocks for ki in range(len(kcs))]
            pending = []
            obh, obhk = obhr.next()

            def emit_S(u):
                q0, kcs, ki = units[u]
                kc = kcs[ki]
                bsel = s_cnt[0] % 2
                s_cnt[0] += 1
                sb, sk = self.pp[bsel].rearrange("p (j q) -> p j q", j=2)[:, :, 0:256], f"ps_s{bsel}"
                qkeys = [f"{qkey}{q0 // 128}", f"{qkey}{q0 // 128 + 1}"]
                for j in range(int(os.environ.get("CJ", "2"))):
                    self.MM(sb[:, j, :], kt[j * 64:(j + 1) * 64, kc * 128:(kc + 1) * 128],
                            QT[j * 64:(j + 1) * 64, idx, q0:q0 + 256], True, True, [ktk] + qkeys, [sk])
                return sb, sk

            _cstage = int(os.environ.get("CSTAGE", "9"))
            if _cstage == 0:
                continue
            nxt = emit_S(0)
            for u in range(len(units)):
                q0, kcs, ki = units[u]
                kc = kcs[ki]
                sb, sk = nxt
                if u + 1 < len(units):
                    nxt = emit_S(u + 1)
                pt, pk_ = ptr_.next()
                if os.environ.get("CNOEXP", "") == "":
                    self.ACT(pt.rearrange("p (j q) -> p j q", j=2), sb, AF.Exp, [sk], [pk_], scale=0.125)
                for j in range(2 if _cstage >= 2 else 0):
                    for sub in range(2):
                        self.MM(accb[j][sub][:, 0:dv + 1], pt[:, j * 256 + sub * 128:j * 256 + (sub + 1) * 128],
                                v3[:, kc, :], ki == 0, ki == len(kcs) - 1, [pk_, vk], [acck[j][sub]])
                while pending and pending[0][0] <= u:
                    pending.pop(0)[1]()
                if ki == len(kcs) - 1 and _cpost:
                    af, afk = afr.next()
                    for j in range(2):
                        for sub in range(2):
                            self.CP("act" if (j + sub) % 2 == 0 else "dve", af[:, j, sub, 0:dv + 1],
                                    accb[j][sub][:, 0:dv + 1], [acck[j][sub]], [afk])
                    st, stk = str_.next()
                    self.RECIP(st[:, 0:4].rearrange("p (a b) -> p a b", a=2), af[:, :, :, dv], [afk], [stk])
                    ob, obk = obh[:, q0 // 128:q0 // 128 + 2, :], obhk
                    if diff:
                        self.TS("dve", st[:, 4:6], st[:, 2:4], lamneg, None, ALU.mult, None, [stk, "lamt"], [stk])
                        for sub in range(2):
                            tt, ttk = ttr.next()
                            o32, o32k = o32r.next()
                            self.TS("pool", tt, af[:, 1, sub, 0:128], st[:, 4 + sub:5 + sub], None, ALU.mult, None, [afk, stk], [ttk])
                            self.STT("dve", o32, af[:, 0, sub, 0:128], st[:, sub:sub + 1], tt, ALU.mult, ALU.add, [afk, stk, ttk], [o32k])
                            self.ACT(sjunk, o32, AF.Square, [o32k], ["sjunk", stk], accum=st[:, 8 + sub:9 + sub])
                            self.rstd(st[:, 8 + sub:9 + sub], 128.0, st[:, 12 + sub:13 + sub], st[:, 10 + sub:11 + sub], [stk], [stk])
                            self.STT("dve", ob[:, sub, :], o32, st[:, 12 + sub:13 + sub], self.sublnS, ALU.mult, ALU.mult,
                                     [o32k, stk, "sublnS"], [obk])
                    else:
                        for sub in range(2):
                            for j in range(2):
                                self.TS("dve" if j == 0 else "pool", ob[:, sub, j * 64:(j + 1) * 64], af[:, j, sub, 0:64],
                                        st[:, j * 2 + sub:j * 2 + sub + 1], None, ALU.mult, None, [afk, stk], [obk])

            nqt = len(qblocks) * 2
            for g0 in range(0, nqt, 8):
                gn = min(8, nqt - g0)
                bk = 4 + g0 // 8
                pT = pbf(bk)[:, 0:gn * 128].rearrange("p (a b) -> p a b", a=gn)
                for qi in range(gn):
                    self.TR(pT[:, qi, :], obh[:, g0 + qi, :], [obhk], [f"pb{bk}"])
                self.CP("act" if (g0 // 8) % 2 == 0 else "dve", OT[:, idx, g0 * 128:(g0 + gn) * 128], pbf(bk)[:, 0:gn * 128], [f"pb{bk}"],
                        [f"{okey}{g0 + qi}" for qi in range(gn)])
        self.barrier()
        self.dump("OdT", OdT); self.dump("OgT", OgT)
        m.release(mkC)
        if stop_after == "C":
            return
        YfT = QTd.rearrange("p a b -> p (a b)")[:, 0:3 * 2304].rearrange("p (a b) -> p a b", a=3)
        YcT = QTg.rearrange("p a b -> p (a b)")[:, 0:3 * 2304].rearrange("p (a b) -> p a b", a=3)

        mkD = m.mark()
        Usb = m.alloc([34, 384], BF16)
        S.dma(Usb, scr["U"], r=["U"], w=["Usb"])
        ctab = Ring(m, 2, [4, 512], BF16, "ctab")
        stab = Ring(m, 2, [4, 512], BF16, "stab")
        pqr = Ring(m, 6, [512], BF16, "pq")
        dc_v = tabs["dftc"].rearrange("(c p) n -> p c n", p=128)
        ds_v = tabs["dfts"].rearrange("(c p) n -> p c n", p=128)
        for lb in range(4):
            for lcg in range(8):
                ct, ck = ctab.next()
                stt, sk_ = stab.next()
                S.dma(ct, dc_v[:, lcg * 4:(lcg + 1) * 4, lb * 512:(lb + 1) * 512], w=[ck])
                S.dma(stt, ds_v[:, lcg * 4:(lcg + 1) * 4, lb * 512:(lb + 1) * 512], w=[sk_])
                for cc in range(3):
                    for li in range(4):
                        lc = lcg * 4 + li
                        self.MM(pb[cc], Usb[:, lc, cc * 128:(cc + 1) * 128], ct[:, li, :], lc == 0, lc == 31, ["Usb", ck], [f"pb{cc}"])
                        self.MM(pb[3 + cc], Usb[:, lc, cc * 128:(cc + 1) * 128], stt[:, li, :], lc == 0, lc == 31, ["Usb", sk_], [f"pb{3 + cc}"])
            for cc in range(3):
                pq, pqk = pqr.next()
                self.CP("act", pq, pb[cc], [f"pb{cc}"], [pqk])
                qq, qqk = pqr.next()
                self.CP("dve", qq, pb[3 + cc], [f"pb{3 + cc}"], [qqk])
                self.MM(pb[6], self.cbd, pq, True, False, ["cbd", pqk], ["pb6"])
                self.MM(pb[6], self.nsbd, qq, False, True, ["nsbd", qqk], ["pb6"])
                self.CP("act", YfT[:, cc, lb * 512:(lb + 1) * 512], pb[6], ["pb6"], [f"YfT{lb}"])
        if ctx_full:
            cct = m.alloc([2, 256], BF16)
            sct = m.alloc([2, 256], BF16)
            S.dma(cct, io["dftcc"].rearrange("(c p) n -> p c n", p=128), w=["cct"])
            S.dma(sct, io["dftsc"].rearrange("(c p) n -> p c n", p=128), w=["sct"])
            for cc in range(3):
                for lc in range(2):
                    self.MM(pb[0][:, 0:256], Usb[:, 32 + lc, cc * 128:(cc + 1) * 128], cct[:, lc, :], lc == 0, lc == 1, ["Usb", "cct"], ["pb0"])
                    self.MM(pb[1][:, 0:256], Usb[:, 32 + lc, cc * 128:(cc + 1) * 128], sct[:, lc, :], lc == 0, lc == 1, ["Usb", "sct"], ["pb1"])
                pq, pqk = pqr.next()
                self.CP("act", pq[:, 0:256], pb[0][:, 0:256], ["pb0"], [pqk])
                qq, qqk = pqr.next()
                self.CP("dve", qq[:, 0:256], pb[1][:, 0:256], ["pb1"], [qqk])
                self.MM(pb[6][:, 0:256], self.cbd, pq[:, 0:256], True, False, ["cbd", pqk], ["pb6"])
                self.MM(pb[6][:, 0:256], self.nsbd, qq[:, 0:256], False, True, ["nsbd", qqk], ["pb6"])
                self.CP("act", YfT[:, cc, 2048:2304], pb[6][:, 0:256], ["pb6"], ["YfT4"])
        self.barrier()
        self.dump("YfT", YfT)
        m.release(mkD)

        if stop_after == "D":
            return
        mkE = m.mark()
        Dm = m.alloc([93, 128], BF16)
        identf = m.alloc([128], F32)
        self.CP("dve", identf, self.ident, ["ident"], ["identf"])
        dwfm = self.fmv[:, 16:109].rearrange("p (a b) -> p a b", a=3)
        for cc in range(3):
            for k in range(31):
                self.TS("dve", Dm[:, cc * 31 + k, :], identf, dwfm[:, cc, k:k + 1], None, ALU.mult, None,
                        ["identf", "fmv"], ["Dm"])
        ybr = Ring(m, 4, [384], F32, "yb")
        ycr = Ring(m, 4, [384], F32, "yc")
        ysr = Ring(m, 4, [384], BF16, "ys")
        st5 = Ring(m, 4, [16], F32, "st5")
        cjunk = m.alloc([384], BF16)
        pr = PRing(pb, [0, 1, 2, 3])
        caps_e = []
        for s in range(n_slots):
            S.capture_start()
            caps_e.append(S._cap)
            if s < 16:
                src, base, sk_ = vT, s * 128, "vT"
            else:
                src, base, sk_ = vTc, (s - 16) * 128, "vTc"
            ps, pk = pr.next()
            for cc in range(3):
                for k in range(31):
                    self.MM(ps[:, cc * 128:(cc + 1) * 128], src[:, cc, base + k:base + k + 128], Dm[:, cc * 31 + k, :],
                            k == 0, k == 30, [sk_, "Dm"], [pk])
            yb, ybk = ybr.next()
            st, stk = st5.next()
            self.TT("dve", yb, ps[:, 0:384], cbias_row, ALU.add, [pk, "rowsB"], [ybk])
            S.dve(lambda e, st=st, yb=yb: e.tensor_reduce(out=st[:, 0:1], in_=yb, axis=AX.X, op=ALU.add), [ybk], [stk])
            self.TS("dve", st[:, 1:2], st[:, 0:1], -1.0 / 384, None, ALU.mult, None, [stk], [stk])
            yc, yck = ycr.next()
            self.TS("pool", yc, yb, st[:, 1:2], None, ALU.add, None, [ybk, stk], [yck])
            self.ACT(cjunk, yc, AF.Square, [yck], ["cjunk", stk], accum=st[:, 2:3])
            self.rstd(st[:, 2:3], 384.0, st[:, 4:5], st[:, 3:4], [stk], [stk])
            self.STT("dve", yb, yc, st[:, 4:5], clng_row, ALU.mult, ALU.mult, [yck, stk, "rowsB"], [ybk])
            self.TT("pool", yc, yb, clnb_row, ALU.add, [ybk, "rowsB"], [yck])
            ys, ysk = ysr.next()
            self.ACT(ys, yc, AF.Silu, [yck], [ysk])
            pT = pbf(4)[:, 0:384].rearrange("p (a b) -> p a b", a=3)
            for cc in range(3):
                self.TR(pT[:, cc, :], ys[:, cc * 128:(cc + 1) * 128], [ysk], ["pb4"])
            self.CP("act", YcT[:, :, s * 128:(s + 1) * 128], pT, ["pb4"], [f"YcT{s}"])
        S.capture_end()
        S.replay_interleaved(caps_e)
        self.barrier()
        self.dump("YcT", YcT)
        m.release(mkE)

        if stop_after == "E":
            return
        mkF = m.mark()
        mT = m.alloc([8, 2304], BF16)
        mkFw = m.mark()
        wgr = Ring(m, 2, [8, 4, 128], BF16, "wg")
        wbr = Ring(m, 2, [14, 128], BF16, "wb")
        sgr = Ring(m, 2, [512], F32, "sgf")
        tmr = Ring(m, 2, [512], F32, "tmf")
        macc = m.alloc([512], F32)
        gr = PRing(pb, [0, 1, 2, 3])
        br_ = PRing(pb, [4, 5, 6, 7])
        brw = [("w_brf", 3, YfT, "YfT"), ("w_brd", 4, OdT, "OdT"), ("w_brg", 4, OgT, "OgT"), ("w_brc", 3, YcT, "YcT")]
        tblocks = [(i * 512, 512) for i in range(4)] + ([(2048, 256)] if ctx_full else [])
        for fc in range(8):
            wg, wgk = wgr.next()
            wb, wbk = wbr.next()
            for bi in range(4):
                c0 = GATE_COL0 + bi * 1024 + fc * 128
                S.dma(wg[:, :, bi, :], win_v[:, :, c0:c0 + 128], w=[wgk], q="pool")
            ko = 0
            for (nm, nk_, _, _) in brw:
                S.dma(wb[:, ko:ko + nk_, :], io[nm].rearrange("(c p) n -> p c n", p=128)[:, :, fc * 128:(fc + 1) * 128],
                      w=[wbk], q="pool")
                ko += nk_
            for (t0, tn) in tblocks:
                hkeys = [f"hT{t0 // 128 + i}" for i in range(tn // 128)]
                ko = 0
                for bi, (nm, nk_, Y, ykey) in enumerate(brw):
                    pg_, pgk = gr.next()
                    for c in range(8):
                        self.MM(pg_[:, 0:tn], wg[:, c, bi, :], hT[:, c, t0:t0 + tn], c == 0, c == 7, [wgk] + hkeys, [pgk])
                    pbr, pbk = br_.next()
                    if ykey in ("OdT", "OgT", "YcT"):
                        ykeys = [f"{ykey}{t0 // 128 + i}" for i in range(tn // 128)]
                    else:
                        ykeys = [f"YfT{t0 // 512}"]
                    for kc in range(nk_):
                        self.MM(pbr[:, 0:tn], wb[:, ko + kc, :], Y[:, kc, t0:t0 + tn], kc == 0, kc == nk_ - 1, [wbk] + ykeys, [pbk])
                    ko += nk_
                    sg, sgk = sgr.next()
                    self.ACT(sg[:, 0:tn], pg_[:, 0:tn], AF.Sigmoid, [pgk], [sgk])
                    if bi == 0:
                        self.TT("dve", macc[:, 0:tn], pbr[:, 0:tn], sg[:, 0:tn], ALU.mult, [pbk, sgk], ["macc"])
                    else:
                        tm, tmk = tmr.next()
                        self.TT("dve", tm[:, 0:tn], pbr[:, 0:tn], sg[:, 0:tn], ALU.mult, [pbk, sgk], [tmk])
                        if bi < 3:
                            self.TT("pool", macc[:, 0:tn], macc[:, 0:tn], tm[:, 0:tn], ALU.add, ["macc", tmk], ["macc"])
                        else:
                            self.TT("pool", mT[:, fc, t0:t0 + tn], macc[:, 0:tn], tm[:, 0:tn], ALU.add, ["macc", tmk],
                                    [f"mT{t0 // 512}"])
        self.barrier()
        self.dump("mT", mT)
        m.release(mkFw)

        if stop_after == "F":
            return
        m_main = m
        m = Mem(self.arena, mkF)
        m.off = mkB2
        Wo = m.alloc([8, 1024], BF16)
        S.dma(Wo, io["w_out"].rearrange("(c p) n -> p c n", p=128), w=["Wo"], q="pool")
        g1 = [m.alloc([1024], F32) for _ in range(2)]
        for j in range(2):
            S.dma(g1[j], scr["gates"][j], r=[f"gates{j}"], w=[f"g1{j}"])
        rings = {
            "st": Ring(m, 6, [32], F32, "st"),
            "junk": Ring(m, 2, [1024], BF16, "junk"),
            "xn": Ring(m, 3, [1024], BF16, "xn"),
        }
        xr = Ring(m, 3, [1024], F32, "xres")
        yr = Ring(m, 3, [1024], F32, "yy")
        xmr = Ring(m, 3, [1024], F32, "xm")
        pr = PRing(pb, [2, 3, 4, 5, 6, 7])

        def post_norm_res(pss, pks, gate, gatek, xres, xresk, xout, xoutk, rings):
            st, stk = rings["st"].next()
            junk, jk = rings["junk"].next()
            for hf in range(2):
                self.ACT(junk[:, 0:512], pss[hf], AF.Square, [pks[hf]], [jk, stk], accum=st[:, 4 + hf:5 + hf])
            self.TT("dve", st[:, 0:1], st[:, 4:5], st[:, 5:6], ALU.add, [stk], [stk])
            self.rstd(st[:, 0:1], 1024.0, st[:, 2:3], st[:, 1:2], [stk], [stk])
            y, yk = yr.next()
            for hf in range(2):
                self.STT("dve", y[:, hf * 512:(hf + 1) * 512], pss[hf], st[:, 2:3], gate[:, hf * 512:(hf + 1) * 512],
                         ALU.mult, ALU.mult, [pks[hf], stk, gatek], [yk])
            self.TT("pool", xout, xres, y, ALU.add, [xresk, yk], [xoutk])

        caps_f2 = []
        for s in range(n_slots):
            S.capture_start()
            caps_f2.append(S._cap)
            j = 0 if s < 16 else 1
            row0 = s * 128 if s < 16 else 4096 + (s - 16) * 128
            pss, pks = [], []
            for hf in range(2):
                ps, pk = pr.next()
                for c in range(8):
                    self.MM(ps, mT[:, c, s * 128:(s + 1) * 128], Wo[:, c, hf * 512:(hf + 1) * 512], c == 0, c == 7,
                            [f"mT{s // 4}", "Wo"], [pk])
                pss.append(ps)
                pks.append(pk)
            xres, xresk = xr.next()
            S.dma(xres, x_all[row0:row0 + 128, :], w=[xresk])
            xm, xmk = xmr.next()
            post_norm_res(pss, pks, g1[j], f"g1{j}", xres, xresk, xm, xmk, rings)
            S.dma(scr["xmid"][s * 128:(s + 1) * 128, :], xm, r=[xmk], w=[f"xmid{s}"])
            norm_mod_tile(xm, xmk, 1, j, hT[:, :, s * 128:(s + 1) * 128], f"hT{s}", rings)
        S.capture_end()
        S.replay_interleaved(caps_f2)
        self.barrier()
        m = m_main
        m.release(mk_layer)

        if stop_after == "F2":
            return
        mkG = m.mark()
        W2 = m.alloc([32, 1024], BF16)
        w2v = io["w_ff2"].rearrange("(c p) n -> p c n", p=128)
        for g in range(8):
            S.dma(W2[:, g * 4:(g + 1) * 4, :], w2v[:, g * 4:(g + 1) * 4, :], w=["W2"], q="pool")
        g2 = [m.alloc([1024], F32) for _ in range(2)]
        for j in range(2):
            S.dma(g2[j], scr["gates"][2 + j], r=[f"gates{2 + j}"], w=[f"g2{j}"])
        aT = m.alloc([32, 512], BF16)
        w1r = Ring(m, 3, [8, 512], BF16, "w1")
        rlr = Ring(m, 2, [512], F32, "rl")
        rings = {
            "st": Ring(m, 4, [32], F32, "st"),
            "junk": Ring(m, 1, [1024], BF16, "junk"),
        }
        xr = Ring(m, 2, [1024], F32, "xres")
        yr = Ring(m, 2, [1024], F32, "yy")
        xor_ = Ring(m, 2, [1024], F32, "xo")
        p1 = PRing(pb, [0, 1, 2, 3])
        p2 = PRing(pb, [4, 5, 6, 7])
        W1b_v = scr["W1b"].rearrange("(c p) n -> p c n", p=128)
        for (t0, tn) in tblocks:
            hkeys = [f"hT{t0 // 128 + i}" for i in range(tn // 128)]
            for fg in range(8):
                w1, w1k = w1r.next()
                S.dma(w1, W1b_v[:, :, fg * 512:(fg + 1) * 512], r=["W1b"], w=[w1k])
                for fi in range(4):
                    fch = fg * 4 + fi
                    ps, pk = p1.next()
                    for c in range(8):
                        self.MM(ps[:, 0:tn], w1[:, c, fi * 128:(fi + 1) * 128], hT[:, c, t0:t0 + tn], c == 0, c == 7,
                                [w1k] + hkeys, [pk])
                    rl, rlk = rlr.next()
                    self.ACT(rl[:, 0:tn], ps[:, 0:tn], AF.Relu, [pk], [rlk])
                    self.TT("pool" if fch % 2 == 0 else "dve", aT[:, fch, 0:tn], rl[:, 0:tn], rl[:, 0:tn], ALU.mult, [rlk], ["aT"])
            for ti in range(tn // 128):
                s = t0 // 128 + ti
                j = 0 if s < 16 else 1
                pss, pks = [], []
                for hf in range(2):
                    ps, pk = p2.next()
                    for fch in range(32):
                        self.MM(ps, aT[:, fch, ti * 128:(ti + 1) * 128], W2[:, fch, hf * 512:(hf + 1) * 512], fch == 0, fch == 31,
                                ["aT", "W2"], [pk])
                    pss.append(ps)
                    pks.append(pk)
                xres, xresk = xr.next()
                S.dma(xres, scr["xmid"][s * 128:(s + 1) * 128, :], r=[f"xmid{s}"], w=[xresk])
                xo, xok = xor_.next()
                post_norm_res(pss, pks, g2[j], f"g2{j}", xres, xresk, xo, xok, rings)
                if s < 16:
                    S.dma(x_out[s * 128:(s + 1) * 128, :], xo, r=[xok], w=[f"xout{s}"])
                    if scatter_to is not None:
                        idx_ap = self.idx_s[:, s:s + 1]
                        S.op("pool", lambda e, xo=xo, idx_ap=idx_ap: e.indirect_dma_start(
                            out=scatter_to[:, :], out_offset=bass.IndirectOffsetOnAxis(ap=idx_ap, axis=0),
                            in_=xo, in_offset=None), [xok, "idx_s"], ["SHXw"], dma=True)
                else:
                    S.dma(xc_out[(s - 16) * 128:(s - 15) * 128, :], xo, r=[xok], w=[f"xcout{s}"])
        self.barrier()
        m.release(mkG)


def _bf(a):
    return np.ascontiguousarray(a.astype(ml_dtypes.bfloat16))


_CONST_CACHE = {}


def host_consts():
    if _CONST_CACHE:
        return _CONST_CACHE
    out = {}
    inv_freq = (10000.0 ** (-np.arange(16, dtype=np.float64) / 16))
    nat = np.arange(T_LAT)
    for half in range(2):
        pos = (nat + half * 2048) % T_LAT
        row = (pos // 64).astype(np.float64)
        col = (pos % 64).astype(np.float64)
        ang = np.stack([row[:, None] * inv_freq, col[:, None] * inv_freq], axis=1)
        cos = np.cos(ang).reshape(T_LAT, 32).astype(np.float32)
        sin = np.sin(ang).reshape(T_LAT, 32).astype(np.float32)
        out[f"cos{half}"] = np.ascontiguousarray(cos.reshape(32, 128, 32).transpose(1, 0, 2))
        out[f"sin{half}"] = np.ascontiguousarray(sin.reshape(32, 128, 32).transpose(1, 0, 2))
        lq = (np.arange(T_OWN) + half * 2048).astype(np.int64)
        prod = (pos.astype(np.int64)[:, None] * lq[None, :]) % T_LAT
        angd = 2.0 * np.pi * prod.astype(np.float64) / T_LAT
        out[f"dftc{half}"] = _bf(np.cos(angd) / 64.0)
        out[f"dfts{half}"] = _bf(np.sin(angd) / 64.0)
        out[f"dftcs{half}"] = np.ascontiguousarray(np.concatenate([out[f"dftc{half}"][2048:], out[f"dftc{half}"][:2048]], axis=0))
        out[f"dftss{half}"] = np.ascontiguousarray(np.concatenate([out[f"dfts{half}"][2048:], out[f"dfts{half}"][:2048]], axis=0))
        mk = np.zeros((128, 4), np.float32)
        mk[:, 0] = float(half)
        mk[:, 1] = float(1 - half)
        mk[:, 2] = float(half)
        mk[:, 3] = float(1 - half)
        out[f"masks{half}"] = mk
    pc = (np.arange(256)[:, None] * np.arange(256)[None, :]) % 256
    angc = 2.0 * np.pi * pc / 256.0
    out["dftcc"] = _bf(np.cos(angc) / 16.0)
    out["dftsc"] = _bf(np.sin(angc) / 16.0)
    p64 = (np.arange(64)[:, None] * np.arange(64)[None, :]) % 64
    a64 = 2.0 * np.pi * p64 / 64.0
    cbd = np.zeros((128, 128))
    sbd = np.zeros((128, 128))
    for g in range(2):
        cbd[g * 64:(g + 1) * 64, g * 64:(g + 1) * 64] = np.cos(a64) / 8.0
        sbd[g * 64:(g + 1) * 64, g * 64:(g + 1) * 64] = -np.sin(a64) / 8.0
    out["cbd"] = _bf(cbd)
    out["nsbd"] = _bf(sbd)
    out["ident"] = _bf(np.eye(128))
    _CONST_CACHE.update(out)
    return out


LAYER_W = ["wmod", "w_in", "w_brf", "w_brd", "w_brg", "w_brc", "w_out", "w_ff1", "w_ff2"]


def layer_host_inputs(l, inp):
    d = {}
    d["wmod"] = np.ascontiguousarray(inp["w_mod"][l])
    d["w_in"] = np.ascontiguousarray(inp["w_in"][l])
    d["w_brf"] = np.ascontiguousarray(inp["w_br_fourier"][l])
    d["w_brd"] = np.ascontiguousarray(inp["w_br_diff"][l])
    d["w_brg"] = np.ascontiguousarray(inp["w_br_gqa"][l])
    d["w_brc"] = np.ascontiguousarray(inp["w_br_conv"][l])
    d["w_out"] = np.ascontiguousarray(inp["w_out"][l])
    d["w_ff1"] = np.ascontiguousarray(inp["w_ff1"][l])
    d["w_ff2"] = np.ascontiguousarray(inp["w_ff2"][l])
    bm = inp["b_mod"][l]
    rowsA = np.concatenate([inp["g_post_mix"][l], inp["g_post_mlp"][l], bm[2048:3072], bm[5120:6144]])
    d["rowsA"] = np.ascontiguousarray(np.broadcast_to(rowsA[None, :], (128, 4096)))
    rowsB = np.concatenate([inp["q_norm"][l], inp["k_norm"][l], inp["diff_subln"][l], inp["conv_dw_bias"][l],
                            inp["conv_ln_g"][l], inp["conv_ln_b"][l], inp["diff_lambda"][l].reshape(-1)])
    d["rowsB"] = np.ascontiguousarray(np.broadcast_to(rowsB[None, :], (128, 1664)))
    d["bmodfm"] = np.ascontiguousarray(bm.reshape(48, 128).T)
    fm = np.zeros((128, 109), np.float32)
    fm[:, 0:8] = inp["g_pre_mix"][l].reshape(8, 128).T
    fm[:, 8:16] = inp["g_pre_mlp"][l].reshape(8, 128).T
    dw = inp["conv_dw"][l]
    fm[:, 16:109] = dw.reshape(31, 3, 128).transpose(2, 1, 0).reshape(128, 93)
    d["fmv"] = fm
    return d


def declare_layer_io(nc, sfx, with_dft_ctx=True):
    io = {}

    def din(name, shape, dt):
        io[name] = nc.dram_tensor(name + sfx, shape, dt, kind="ExternalInput").ap()
    din("wmod", [1024, 6144], F32)
    din("w_in", [1024, IN_COLS], F32)
    din("w_brf", [384, 1024], F32)
    din("w_brd", [512, 1024], F32)
    din("w_brg", [512, 1024], F32)
    din("w_brc", [384, 1024], F32)
    din("w_out", [1024, 1024], F32)
    din("w_ff1", [1024, 4096], F32)
    din("w_ff2", [4096, 1024], F32)
    din("rowsA", [128, 4096], F32)
    din("rowsB", [128, 1664], F32)
    din("bmodfm", [128, 48], F32)
    din("fmv", [128, 109], F32)
    return io


def declare_common_io(nc):
    io = {}

    def din(name, shape, dt):
        io[name] = nc.dram_tensor(name, shape, dt, kind="ExternalInput").ap()
    din("sc2", [128, 8, 2], F32)
    din("cos", [128, 32, 32], F32)
    din("sin", [128, 32, 32], F32)
    din("masks", [128, 4], F32)
    din("dftc", [4096, 2048], BF16)
    din("dfts", [4096, 2048], BF16)
    din("dftcc", [256, 256], BF16)
    din("dftsc", [256, 256], BF16)
    din("cbd", [128, 128], BF16)
    din("nsbd", [128, 128], BF16)
    din("ident", [128, 128], BF16)
    return io


def declare_scratch(nc, sfx=""):
    import os
    kind = "ExternalOutput" if os.environ.get("DBG", "") != "" else "Internal"

    def dsc(name, shape, dt):
        return nc.dram_tensor(name + sfx, shape, dt, kind=kind).ap()
    return {
        "KTd": dsc("KTd", [4, 128, NKEY], BF16),
        "Vd": dsc("Vd", [4, 128, 34, 129], BF16),
        "KTg": dsc("KTg", [2, 128, NKEY], BF16),
        "Vg": dsc("Vg", [2, 128, 34, 65], BF16),
        "U": dsc("U", [128, 34, 384], BF16),
        "gates": dsc("gates", [4, 128, 1024], F32),
        "xmid": dsc("xmid", [2304, 1024], F32),
        "W1b": dsc("W1b", [1024, 4096], BF16),
    }


def build_single_layer(l, stop_after=None):
    nc = bass.Bass("TRN2", target_bir_lowering=False)
    io = declare_common_io(nc)
    io.update(declare_layer_io(nc, ""))
    x_all = nc.dram_tensor("x_all", [NKEY, 1024], F32, kind="ExternalInput").ap()
    x_out = nc.dram_tensor("x_out", [T_OWN, 1024], F32, kind="ExternalOutput").ap()
    ctx_full = l < DEPTH - 1
    xc_out = nc.dram_tensor("xc_out", [T_CTX, 1024], F32, kind="ExternalOutput").ap() if ctx_full else None
    scr = declare_scratch(nc)
    B = Builder(nc)
    B.load_consts(io)
    B.layer(l, io, x_all, x_out, xc_out, scr, ctx_full, stop_after=stop_after)
    B.S.emit(nc)
    return nc


def common_core_inputs(inp, core):
    b, half = core // 2, core % 2
    hc = host_consts()
    sc = np.stack([inp["c"][b], inp["c_ctx"]], axis=1)
    sc2 = np.ascontiguousarray(sc.reshape(8, 128, 2).transpose(1, 0, 2))
    return {
        "sc2": sc2,
        "cos": hc[f"cos{half}"], "sin": hc[f"sin{half}"], "masks": hc[f"masks{half}"],
        "dftc": hc[f"dftc{half}"], "dfts": hc[f"dfts{half}"],
        "dftcc": hc["dftcc"], "dftsc": hc["dftsc"], "cbd": hc["cbd"], "nsbd": hc["nsbd"], "ident": hc["ident"],
    }


def build_fused():
    nc = bass.Bass("TRN2", target_bir_lowering=False)
    io = declare_common_io(nc)
    io["idxs"] = nc.dram_tensor("idxs", [128, 33], I32, kind="ExternalInput").ap()
    io["tok"] = nc.dram_tensor("tok", [128, 64], I32, kind="ExternalInput").ap()
    tabsA = {k: io[k] for k in ("cos", "sin", "dftc", "dfts", "masks")}
    io0 = dict(io)
    io0.update(declare_layer_io(nc, "_l0"))
    io1 = dict(io)
    io1.update(declare_layer_io(nc, "_l1"))
    x_all = nc.dram_tensor("x_all", [NKEY, 1024], F32, kind="ExternalInput").ap()
    out = nc.dram_tensor("x_out", [T_OWN, 1024], F32, kind="ExternalOutput").ap()
    X1 = nc.dram_tensor("X1", [NKEY, 1024], F32, kind="Internal").ap()
    SHX = nc.dram_tensor("SHX", [T_LAT, 1024], F32, kind="Internal", addr_space="Shared").ap()
    FLG = nc.dram_tensor("FLG", [256, 64], I32, kind="Internal", addr_space="Shared").ap()
    scr = declare_scratch(nc)
    B = Builder(nc)
    B.load_consts(io)
    S = B.S
    B.layer(0, io0, x_all, X1[0:2048, :], X1[4096:NKEY, :], scr, True, tabs=tabsA, scatter_to=SHX)
    S.op("pool", lambda e: e.indirect_dma_start(
        out=FLG[:, :], out_offset=bass.IndirectOffsetOnAxis(ap=B.idx_s[:, 32:33], axis=0),
        in_=B.tok_s, in_offset=None), ["idx_s", "tok_s"], ["FLGw"], dma=True)

    def poll(g):
        with g.register("tk") as tk, g.register("f0") as f0, g.register("dd") as dd:
            g.reg_load(tk, B.tok_s[0:1, 0:1])
            for row in (0, 128):
                g.reg_mov(dd, 1)
                with g.While(dd):
                    g.reg_load(f0, FLG[row:row + 1, 0:1])
                    g.reg_sub(dd, f0, tk)
        return g.memset(B.bar2, 0.0)
    S.op("pool", poll, ["FLGw", "tok_s"], ["SHXready"])
    B.layer(1, io1, X1, out, None, scr, False, tabs=tabsA, gather_from=SHX)
    S.emit(nc)
    return nc


_NC_CACHE = {}


def kernel(**inputs):
    inp = {k: np.asarray(v) for k, v in inputs.items()}
    x = inp["x"].astype(np.float32, copy=False)
    xc = inp["ctx"].astype(np.float32, copy=False)
    n = 8
    if "nc" not in _NC_CACHE:
        _NC_CACHE["nc"] = build_fused()
    nc = _NC_CACHE["nc"]
    lw0 = layer_host_inputs(0, inp)
    lw1 = layer_host_inputs(1, inp)
    tok = np.full((128, 64), int(np.random.default_rng().integers(1, 2 ** 30)), np.int32)
    hc = host_consts()
    in_maps = []
    for core in range(n):
        b, half = core // 2, core % 2
        own = x[b, half * 2048:(half + 1) * 2048]
        oth = x[b, (1 - half) * 2048:(2 - half) * 2048]
        mp = dict(common_core_inputs(inp, core))
        for k, v in lw0.items():
            mp[k + "_l0"] = v
        for k, v in lw1.items():
            mp[k + "_l1"] = v
        idxs = np.zeros((128, 33), np.int32)
        pp_ = np.arange(128, dtype=np.int32)
        for s_ in range(16):
            idxs[:, s_] = half * 2048 + s_ * 128 + pp_
            idxs[:, 16 + s_] = (1 - half) * 2048 + s_ * 128 + pp_
        idxs[:, 32] = half * 128 + pp_
        mp["idxs"] = idxs
        mp["tok"] = tok
        mp["x_all"] = np.ascontiguousarray(np.concatenate([own, oth, xc[b]], axis=0))
        in_maps.append(mp)
    res = run_bass_kernel_spmd(nc, in_maps, core_ids=list(range(n)))
    out = np.empty_like(x)
    for core in range(n):
        b, half = core // 2, core % 2
        out[b, half * 2048:(half + 1) * 2048] = res.results[core]["x_out"]
    return out
```
